# Optimizing a Trainium2 kernel written in Bass

```python
import numpy as np
import jax
import jax.numpy as jnp
from jax import lax

D_MODEL = 1024
BATCH = 8
SEQ = 2048
DEPTH = 2

D_MIX = D_MODEL
MLSTM_HEADS = 4
MLSTM_HEAD_DIM = D_MIX // 8
SB_HEADS = 4
SB_HEAD_DIM = D_MIX // 16
POOL_GROUPS = 4
POOL_CH = D_MIX // 16
POOL_WINDOWS = (2, 4, 8, 16)
W_M = MLSTM_HEADS * MLSTM_HEAD_DIM
W_SB = SB_HEADS * SB_HEAD_DIM
W_P = POOL_GROUPS * POOL_CH
MLSTM_CHUNK = 64
MLSTM_CONV = 4
SB_BLOCK = 128
FFN_CONV = 3
D_FF = ((8 * D_MODEL // 3 + 127) // 128) * 128
EPS = 1e-6
SPLIT_SIZES = (W_M, W_M, W_M, W_M, MLSTM_HEADS, MLSTM_HEADS, W_SB, W_SB, W_SB, W_P)
D_IN = sum(SPLIT_SIZES)

kernel_name = 'hybrid_mlstm_stickbreak_pool_block'


def rmsnorm(x, g):
    x32 = x.astype(jnp.float32)
    y = x32 * lax.rsqrt(jnp.mean(x32 * x32, axis=-1, keepdims=True) + EPS)
    return (y * g.astype(jnp.float32)).astype(x.dtype)


def causal_dwconv(x, w, b):
    k_width = w.shape[0]
    s = x.shape[1]
    xp = jnp.pad(x, ((0, 0), (k_width - 1, 0), (0, 0)))
    y = b
    for j in range(k_width):
        y = y + xp[:, j:j + s] * w[j]
    return y


def _mlstm_chunk(carry, xs):
    c_mat, n_vec, m_prev = carry
    q, k, v, log_f, i_log = xs
    L = q.shape[2]
    causal = jnp.tril(jnp.ones((L, L), dtype=bool))
    b = jnp.cumsum(log_f, axis=-1)
    d = jnp.where(causal, b[..., :, None] - b[..., None, :] + i_log[..., None, :], -jnp.inf)
    inter = b + m_prev[..., None]
    m_t = jnp.maximum(inter, jnp.max(d, axis=-1))
    w_inter = jnp.exp(inter - m_t)
    scores = jnp.einsum('bhtd,bhsd->bhts', q, k) * jnp.exp(d - m_t[..., None])
    num = (w_inter[..., None] * jnp.einsum('bhtd,bhde->bhte', q, c_mat)
           + jnp.einsum('bhts,bhse->bhte', scores, v))
    den = w_inter * jnp.einsum('bhtd,bhd->bht', q, n_vec) + jnp.sum(scores, axis=-1)
    h = num / jnp.maximum(jnp.abs(den), jnp.exp(-m_t))[..., None]
    b_end = b[..., -1]
    g = b_end[..., None] - b + i_log
    m_new = jnp.maximum(b_end + m_prev, jnp.max(g, axis=-1))
    decay = jnp.exp(b_end + m_prev - m_new)
    w_s = jnp.exp(g - m_new[..., None])
    c_new = decay[..., None, None] * c_mat + jnp.einsum('bhs,bhsd,bhse->bhde', w_s, k, v)
    n_new = decay[..., None] * n_vec + jnp.einsum('bhs,bhsd->bhd', w_s, k)
    return (c_new, n_new, m_new), h


def mlstm(q, k, v, i_pre, f_pre):
    bsz, s, h, dh = q.shape
    nc = s // MLSTM_CHUNK

    def to_chunks(a):
        a = a.astype(jnp.float32).reshape((bsz, nc, MLSTM_CHUNK, h) + a.shape[3:])
        return jnp.moveaxis(a, (1, 3), (0, 2))

    xs = (to_chunks(q), to_chunks(k) * (dh ** -0.5), to_chunks(v),
          to_chunks(jax.nn.log_sigmoid(f_pre.astype(jnp.float32))), to_chunks(i_pre))
    init = (jnp.zeros((bsz, h, dh, dh), jnp.float32),
            jnp.zeros((bsz, h, dh), jnp.float32),
            jnp.zeros((bsz, h), jnp.float32))
    _, hs = lax.scan(_mlstm_chunk, init, xs)
    hs = jnp.moveaxis(hs, (0, 2), (1, 3))
    return hs.reshape(bsz, s, h * dh)


def stick_breaking(q, k, v):
    bsz, s, h, d = q.shape
    qh = jnp.transpose(q, (0, 2, 1, 3)).astype(jnp.float32) * (d ** -0.5)
    kh = jnp.transpose(k, (0, 2, 1, 3)).astype(jnp.float32)
    vh = jnp.transpose(v, (0, 2, 1, 3)).astype(jnp.float32)
    s_idx = jnp.arange(s)

    def block(i):
        start = i * SB_BLOCK
        qb = lax.dynamic_slice_in_dim(qh, start, SB_BLOCK, axis=2)
        z = jnp.einsum('bhtd,bhsd->bhts', qb, kh)
        t_idx = start + jnp.arange(SB_BLOCK)
        mask = s_idx[None, :] < t_idx[:, None]
        log_1mb = jnp.where(mask, jax.nn.log_sigmoid(-z), 0.0)
        after = lax.cumsum(log_1mb, axis=3, reverse=True) - log_1mb
        a = jnp.where(mask, jnp.exp(jax.nn.log_sigmoid(z) + after), 0.0)
        return jnp.einsum('bhts,bhsd->bhtd', a, vh)

    out = lax.map(block, jnp.arange(s // SB_BLOCK))
    out = jnp.transpose(out, (1, 0, 3, 2, 4))
    return out.reshape(bsz, s, h * d)


def pool_mixer(u, pool_w, pool_scale):
    bsz, s, _ = u.shape
    ug = u.astype(jnp.float32).reshape(bsz, s, POOL_GROUPS, POOL_CH)
    cs = jnp.cumsum(ug, axis=1)
    pos = jnp.arange(s, dtype=jnp.float32)
    outs = []
    for g, w in enumerate(POOL_WINDOWS):
        c = cs[:, :, g]
        lag = jnp.pad(c, ((0, 0), (w, 0), (0, 0)))[:, :s]
        cnt = jnp.minimum(pos + 1.0, float(w))[None, :, None]
        outs.append((c - lag) / cnt - ug[:, :, g])
    y = jnp.stack(outs, axis=2)
    y = jnp.einsum('bsgc,gce->bsge', y, pool_w.astype(jnp.float32))
    return (y.reshape(bsz, s, W_P) * pool_scale).astype(u.dtype)


def setup_inputs(seed: int = 0) -> dict:
    key = jax.random.key(seed)
    ks = jax.random.split(key, 21)
    nrm = jax.random.normal

    def gain(k, n):
        return 1.0 + 0.05 * nrm(k, (DEPTH, n), jnp.float32)

    return {
        'x': nrm(ks[0], (BATCH, SEQ, D_MODEL), jnp.float32),
        'pre_mix_g': gain(ks[1], D_MODEL),
        'w_in': nrm(ks[2], (DEPTH, D_MODEL, D_IN), jnp.float32) * D_MODEL ** -0.5,
        'mlstm_conv_w': nrm(ks[3], (DEPTH, MLSTM_CONV, 2 * W_M), jnp.float32) * MLSTM_CONV ** -0.5,
        'mlstm_conv_b': 0.01 * nrm(ks[4], (DEPTH, 2 * W_M), jnp.float32),
        'i_bias': 0.1 * nrm(ks[5], (DEPTH, MLSTM_HEADS), jnp.float32),
        'f_bias': jnp.linspace(3.0, 6.0, MLSTM_HEADS)[None, :] + 0.1 * nrm(ks[6], (DEPTH, MLSTM_HEADS), jnp.float32),
        'pool_w': nrm(ks[7], (DEPTH, POOL_GROUPS, POOL_CH, POOL_CH), jnp.float32) * POOL_CH ** -0.5,
        'pool_scale': 1.0 + 0.1 * nrm(ks[8], (DEPTH, W_P), jnp.float32),
        'mlstm_out_g': gain(ks[9], W_M),
        'sb_out_g': gain(ks[10], W_SB),
        'pool_out_g': gain(ks[11], W_P),
        'w_out': nrm(ks[12], (DEPTH, D_MIX, D_MODEL), jnp.float32) * D_MIX ** -0.5,
        'post_mix_g': gain(ks[13], D_MODEL),
        'pre_ffn_g': gain(ks[14], D_MODEL),
        'ffn_w_up': nrm(ks[15], (DEPTH, D_MODEL, 2 * D_FF), jnp.float32) * D_MODEL ** -0.5,
        'ffn_conv_w': nrm(ks[16], (DEPTH, FFN_CONV, 2 * D_FF), jnp.float32) * FFN_CONV ** -0.5,
        'ffn_conv_b': 0.01 * nrm(ks[17], (DEPTH, 2 * D_FF), jnp.float32),
        'ffn_w_down': nrm(ks[18], (DEPTH, D_FF, D_MODEL), jnp.float32) * D_FF ** -0.5,
        'post_ffn_g': gain(ks[19], D_MODEL),
    }


def reference(x, pre_mix_g, w_in, mlstm_conv_w, mlstm_conv_b, i_bias, f_bias, pool_w, pool_scale,
              mlstm_out_g, sb_out_g, pool_out_g, w_out, post_mix_g, pre_ffn_g, ffn_w_up,
              ffn_conv_w, ffn_conv_b, ffn_w_down, post_ffn_g):
    bsz, s, _ = x.shape
    offsets = np.cumsum(SPLIT_SIZES)[:-1].tolist()
    for l in range(DEPTH):
        h = rmsnorm(x, pre_mix_g[l])
        proj = jnp.einsum('bsd,de->bse', h, w_in[l])
        q_m, k_m, v_m, o_m, i_m, f_m, q_sb, k_sb, v_sb, u_p = jnp.split(proj, offsets, axis=-1)
        qk = jax.nn.silu(causal_dwconv(jnp.concatenate([q_m, k_m], axis=-1), mlstm_conv_w[l], mlstm_conv_b[l]))
        q_m, k_m = jnp.split(qk, 2, axis=-1)
        hm = (bsz, s, MLSTM_HEADS, MLSTM_HEAD_DIM)
        y_m = mlstm(q_m.reshape(hm), k_m.reshape(hm), v_m.reshape(hm), i_m + i_bias[l], f_m + f_bias[l])
        y_m = (jax.nn.sigmoid(o_m.astype(jnp.float32)) * y_m).astype(x.dtype)
        hs = (bsz, s, SB_HEADS, SB_HEAD_DIM)
        y_sb = stick_breaking(q_sb.reshape(hs), k_sb.reshape(hs), v_sb.reshape(hs)).astype(x.dtype)
        y_p = pool_mixer(u_p, pool_w[l], pool_scale[l])
        mix = jnp.concatenate([rmsnorm(y_m, mlstm_out_g[l]), rmsnorm(y_sb, sb_out_g[l]),
                               rmsnorm(y_p, pool_out_g[l])], axis=-1)
        x = x + rmsnorm(jnp.einsum('bse,ed->bsd', mix, w_out[l]), post_mix_g[l])
        h = rmsnorm(x, pre_ffn_g[l])
        up = causal_dwconv(jnp.einsum('bsd,df->bsf', h, ffn_w_up[l]), ffn_conv_w[l], ffn_conv_b[l])
        gate, val = jnp.split(up, 2, axis=-1)
        ffn = jnp.einsum('bsf,fd->bsd', jax.nn.gelu(gate, approximate=True) * val, ffn_w_down[l])
        x = x + rmsnorm(ffn, post_ffn_g[l])
    return x
```

```python
import math
import numpy as np
from contextlib import ExitStack
import concourse.bass as bass
import concourse.mybir as mybir
from concourse.bass_utils import run_bass_kernel_spmd

F32 = mybir.dt.float32
BF16 = mybir.dt.bfloat16
AF = mybir.ActivationFunctionType
ALU = mybir.AluOpType

S = 2048
D = 1024
KC = 8
G = 512
NGR = S // G
DFF = 2816
NF = DFF // 128
DIN = 3080
EPS = 1e-6
NCORES = 8
ENGS = ("pe", "act", "dve", "pool", "sp")

PV_PREMIX, PV_POSTMIX, PV_PREFFN, PV_POSTFFN, PV_MIXG = 0, 8, 16, 24, 32
PV_CWM, PV_CBM, PV_CWF, PV_CBF, PV_PSCALE, PV_INVW = 40, 72, 80, 212, 256, 258
NPV = 260


class Buf:
    __slots__ = ("name", "writer", "readers")

    def __init__(self, name):
        self.name = name
        self.writer = None
        self.readers = []


class Instr:
    __slots__ = ("eng", "fn", "deps", "dwaits", "signal", "sigval", "dma_key", "finish")

    def __init__(self, eng, fn, dma_key=None):
        self.eng = eng
        self.fn = fn
        self.deps = []
        self.dwaits = {}
        self.signal = False
        self.sigval = 0
        self.dma_key = dma_key
        self.finish = 0.0


class Prog:
    def __init__(self, nc, stack):
        self.nc = nc
        self.stack = stack
        self.E = {"pe": nc.tensor, "act": nc.scalar, "dve": nc.vector, "pool": nc.gpsimd, "sp": nc.sync}
        self.instrs = []
        self.dma_count = {}
        self.sems = {}
        self.last = {e: None for e in ENGS}
        self.efree = {e: 0.0 for e in ENGS}
        self.defer = None
        self.bar = {e: None for e in ENGS}

    def _sem(self, key):
        if key not in self.sems:
            self.sems[key] = self.stack.enter_context(self.nc.semaphore("s_" + str(key)))
        return self.sems[key]

    def barrier(self):
        tmax = max(self.efree.values())
        for e in ENGS:
            self.efree[e] = tmax
        lasts = [i for i in self.last.values() if i is not None]
        snap = dict(self.dma_count)
        for e in ENGS:
            self.bar[e] = (lasts, snap)

    def _producers(self, eng, reads, writes, dma_key):
        out = []
        strict = eng != "pe"
        for b in reads:
            if b.writer is not None:
                out.append(b.writer)
        for b in writes:
            w = b.writer
            if w is not None and (w.dma_key is not None or w.eng != eng or dma_key is not None or strict):
                out.append(w)
            for r in b.readers:
                if r.dma_key is not None or r.eng != eng or dma_key is not None or strict:
                    out.append(r)
        return out

    def peek_start(self, desc):
        eng, fn, reads, writes, dma_key, dur = desc
        t = self.efree[eng]
        for p in self._producers(eng, reads, writes, dma_key):
            t = max(t, p.finish + (0.2 if p.eng != eng or p.dma_key is not None else 0.05))
        return t

    def op(self, eng, fn, reads=(), writes=(), dma_key=None, dur=0.5):
        if self.defer is not None:
            self.defer.append((eng, fn, tuple(reads), tuple(writes), dma_key, dur))
            return None
        return self._op(eng, fn, reads, writes, dma_key, dur)

    def _op(self, eng, fn, reads=(), writes=(), dma_key=None, dur=0.5):
        start = self.peek_start((eng, fn, reads, writes, dma_key, dur))
        ins = Instr(eng, fn, dma_key)
        if dma_key is not None:
            self.efree[eng] = start + 0.6
            ins.finish = start + 3.5
        else:
            ins.finish = start + dur
            self.efree[eng] = ins.finish
        deps = {}

        def add(p):
            if p is not None:
                deps[id(p)] = p

        for b in reads:
            add(b.writer)
        strict = eng != "pe"
        for b in writes:
            w = b.writer
            if w is not None and (w.dma_key is not None or w.eng != eng or dma_key is not None or strict):
                add(w)
            for r in b.readers:
                if r.dma_key is not None or r.eng != eng or dma_key is not None or strict:
                    add(r)
        if self.bar[eng] is not None:
            lasts, snap = self.bar[eng]
            self.bar[eng] = None
            for p in lasts:
                if p.eng != eng or dma_key is not None or eng == "pool":
                    add(p)
            for k, c in snap.items():
                ins.dwaits[k] = max(ins.dwaits.get(k, 0), c)
        for p in deps.values():
            if p.dma_key is not None:
                ins.dwaits[p.dma_key] = max(ins.dwaits.get(p.dma_key, 0), self.dma_count[p.dma_key])
            else:
                p.signal = True
                ins.deps.append(p)
        if dma_key is not None:
            self.dma_count[dma_key] = self.dma_count.get(dma_key, 0) + 1
        else:
            self.last[eng] = ins
        for b in writes:
            b.writer = ins
            b.readers = []
        for b in reads:
            if dma_key is None:
                b.readers = [r for r in b.readers if r.dma_key is not None or r.eng != eng]
            b.readers.append(ins)
        self.instrs.append(ins)
        return ins

    def record(self, fn):
        assert self.defer is None
        self.defer = []
        try:
            fn()
            return self.defer
        finally:
            self.defer = None

    def run_streams(self, streams, after=None):
        after = after or {}
        pos = [0] * len(streams)
        while True:
            best = None
            for i, st in enumerate(streams):
                if pos[i] >= len(st):
                    continue
                if any(pos[j] < len(streams[j]) for j in after.get(i, ())):
                    continue
                t = self.peek_start(st[pos[i]])
                if best is None or t < best[0]:
                    best = (t, i)
            if best is None:
                break
            i = best[1]
            self._op(*streams[i][pos[i]])
            pos[i] += 1

    def dma(self, queue, out, in_, reads=(), writes=(), key="dma", **kw):
        return self.op(queue, lambda e: e.dma_start(out=out, in_=in_, **kw), reads, writes, dma_key=key, dur=3.5)

    def emit(self, final_wait_keys=(), final_eng="sp"):
        sigcount = {e: 0 for e in ENGS}
        waited = {e: {} for e in ENGS}
        for ins in self.instrs:
            if ins.dma_key is None and ins.signal:
                sigcount[ins.eng] += 1
                ins.sigval = sigcount[ins.eng]
        nw = 0
        for ins in self.instrs:
            eng = self.E[ins.eng]
            need = {}
            for p in ins.deps:
                k = "eng_" + p.eng
                need[k] = max(need.get(k, 0), p.sigval)
            for dk, c in ins.dwaits.items():
                k = "dma_" + str(dk)
                need[k] = max(need.get(k, 0), 16 * c)
            for k, v in need.items():
                if waited[ins.eng].get(k, 0) >= v:
                    continue
                eng.wait_ge(self._sem(k), v)
                waited[ins.eng][k] = v
                nw += 1
            bi = ins.fn(eng)
            if ins.dma_key is not None:
                bi.then_inc(self._sem("dma_" + str(ins.dma_key)), 16)
            elif ins.signal:
                bi.then_inc(self._sem("eng_" + ins.eng), 1)
        print("[kernel] sigcount", sigcount, "dma", {k: 16 * v for k, v in self.dma_count.items()})
        eng = self.E[final_eng]
        for key in final_wait_keys:
            eng.wait_ge(self._sem("dma_" + str(key)), 16 * self.dma_count[key])
        return nw, len(self.instrs)


class Arena:
    def __init__(self, ap, nwords):
        self.ap = ap
        self.n = nwords
        self.top = 0
        self.peak = 0

    def alloc(self, dtype, shape):
        ne = 1
        for d in shape:
            ne *= d
        nbytes = ne * (2 if dtype == BF16 else 4)
        words = (nbytes + 3) // 4
        words = (words + 7) // 8 * 8
        assert self.top + words <= self.n, f"arena overflow {self.top + words} > {self.n}"
        v = self.ap[:, self.top:self.top + words]
        self.top += words
        self.peak = max(self.peak, self.top)
        if dtype == BF16:
            v = v.bitcast(BF16)
        v = v[:, 0:ne]
        if len(shape) == 2:
            v = v.rearrange("p (a b) -> p a b", a=shape[0])
        elif len(shape) == 3:
            v = v.rearrange("p (a b c) -> p a b c", a=shape[0], b=shape[1])
        return v


class StopBuild(Exception):
    pass


STOP_AT = None
HF_RANGE = (0, 1)
ML_INTERLEAVE = True
SUBBANK = 1
ML_SKIP = False
ML_LIMIT = 0


def chk(name):
    if STOP_AT == name:
        raise StopBuild(name)


class TB:
    def __init__(self, ap, name):
        self.ap = ap
        self.b = Buf(name)
        self.h = None


def build_program(layers, nlayers_total=2, debug=False):
    nc = bass.Bass("TRN2", target_bir_lowering=False)
    dr = {}

    def din(name, shape):
        dr[name] = nc.dram_tensor(name, shape, F32, kind="ExternalInput").ap()
        return dr[name]

    xT_in = din("xT", [D, S])
    for l in layers:
        din(f"wfm{l}", [14, 128, KC * 128])
        din(f"wtm{l}", [128, KC * 1288])
        din(f"wout{l}", [8, 128, KC * 128])
        din(f"wup{l}", [2 * NF, 128, KC * 128])
        din(f"wdn{l}", [8, 128, NF * 128])
        din(f"pvec{l}", [128, NPV])
        din(f"gbias{l}", [128, 32])
        din(f"poolw{l}", [128, 256])
    din("cbf", [128, 1152])
    din("cf32", [128, 288])
    yT = nc.dram_tensor("yT", [D, S], F32, kind="ExternalOutput").ap()
    if debug:
        dbg_mix = nc.dram_tensor("dbg_mix", [D, S], F32, kind="ExternalOutput").ap()
        dbg_mid = nc.dram_tensor("dbg_mid", [D, S], F32, kind="ExternalOutput").ap()
        dbg_act = nc.dram_tensor("dbg_act", [DFF, S], F32, kind="ExternalOutput").ap()
        dbg_h2 = nc.dram_tensor("dbg_h2", [D, S], F32, kind="ExternalOutput").ap()
    xa = nc.dram_tensor("xa", [D, S], F32, kind="Internal").ap()
    xb = nc.dram_tensor("xb", [D, S], F32, kind="Internal").ap()

    with ExitStack() as st:
        P = Prog(nc, st)
        NW = 48640
        arena_t = st.enter_context(nc.sbuf_tensor("arena", [128, NW], F32))
        A = Arena(arena_t, NW)
        pbt = [st.enter_context(nc.psum_tensor(f"pb{i}", [128, 512], F32)) for i in range(8)]
        pb = [TB(t, f"pb{i}") for i, t in enumerate(pbt)]

        def new(dtype, shape, name):
            return TB(A.alloc(dtype, shape), name)

        cbf = new(BF16, [1152], "cbf")
        cf32 = new(F32, [288], "cf32")
        ident = cbf.ap[:, 0:128]
        tri_neg = cbf.ap[:, 128:256]
        ones_neg = cbf.ap[:, 256:384]
        ones_bf = cbf.ap[:, 384:512]
        mask_ml = cbf.ap[:, 512:640]
        sbmask = cbf.ap[:, 640:1152]
        tri_f = cf32.ap[:, 0:128]
        ones_f = cf32.ap[:, 128:256]
        poolcorr = cf32.ap[:, 256:288].rearrange("p (b t) -> p b t", b=2)
        pvec = new(F32, [NPV], "pvec")
        gbias = new(F32, [32], "gbias")
        poolw = new(BF16, [2, 128], "poolw")
        Cst = new(F32, [4, 132], "Cst")
        Cbf = new(BF16, [4, 132], "Cbf")
        hal_m = new(F32, [8, 4], "hal_m")
        hal_f = new(F32, [2 * NF, 2], "hal_f")
        hal_f_b = [Buf(f"hal_f{i}") for i in range(4)]
        hal_p = new(F32, [2, 16], "hal_p")
        k_sb = new(BF16, [2, S], "k_sb")
        v_sb = new(BF16, [S // 128, 256], "v_sb")
        pv = pvec.ap

        P.dma("pool", cbf.ap, dr["cbf"], writes=[cbf.b], key="const")
        P.dma("sp", cf32.ap, dr["cf32"], writes=[cf32.b], key="constf")

        def _fs(ap):
            n = 1
            for d_ in list(ap.shape)[1:]:
                n *= int(d_)
            return n

        def ACT(out, in_, func, reads, writes, scale=1.0, bias=0.0, accum=None):
            kw = {}
            if accum is not None:
                kw["accum_out"] = accum
            du = 0.2 + 0.00095 * _fs(out)
            if func == AF.Copy:
                return P.op("act", lambda e: e.activation(out=out, in_=in_, func=func, scale=scale, **kw), reads, writes, dur=du)
            return P.op("act", lambda e: e.activation(out=out, in_=in_, func=func, scale=scale, bias=bias, **kw), reads, writes, dur=du)

        def TT(eng, out, a, b, op, reads, writes):
            du = (0.12 + 0.0011 * _fs(out)) if eng == "dve" else (0.3 + 0.002 * _fs(out))
            return P.op(eng, lambda e: e.tensor_tensor(out=out, in0=a, in1=b, op=op), reads, writes, dur=du)

        def TS(eng, out, a, s1, s2, op0, op1, reads, writes):
            du = 0.12 + 0.0011 * _fs(out)
            if s2 is None:
                return P.op(eng, lambda e: e.tensor_scalar(out=out, in0=a, scalar1=s1, scalar2=None, op0=op0), reads, writes, dur=du)
            return P.op(eng, lambda e: e.tensor_scalar(out=out, in0=a, scalar1=s1, scalar2=s2, op0=op0, op1=op1), reads, writes, dur=du)

        def STT(out, a, s, b, op0, op1, reads, writes):
            return P.op("dve", lambda e: e.scalar_tensor_tensor(out=out, in0=a, scalar=s, in1=b, op0=op0, op1=op1), reads, writes,
                        dur=0.12 + 0.0012 * _fs(out))

        def CP(eng, out, in_, reads, writes):
            du = (0.1 + 0.0009 * _fs(out)) if eng == "dve" else (0.3 + 0.002 * _fs(out))
            return P.op(eng, lambda e: e.tensor_copy(out=out, in_=in_), reads, writes, dur=du)

        def MS(eng, out, val, writes):
            return P.op(eng, lambda e: e.memset(out, val), [], writes, dur=0.3)

        def MM(out, lhsT, rhs, start, stop, reads, writes, sgc=False):
            return P.op("pe", lambda e: e.matmul(out, lhsT=lhsT, rhs=rhs, start=start, stop=stop, skip_group_check=sgc), reads, writes,
                        dur=max(_fs(out), 64) / 2400.0 * 1.3)

        def TR(out, in_, reads, writes):
            return P.op("pe", lambda e: e.transpose(out, in_, ident), list(reads) + [cbf.b], writes, dur=0.15)

        def xview(ap, q):
            return ap.rearrange("(k p) t -> p k t", p=128)[:, :, q * G:(q + 1) * G]

        class WStream:
            def __init__(self, slots, keybase, hold=1):
                self.hold = hold
                self.slots = slots
                self.keybase = keybase
                self.tasks = []
                self.issued = 0
                self.used = 0

            def add(self, dram_ap, ncols):
                self.tasks.append((dram_ap, ncols))

            def _issue(self):
                i = self.issued
                src, ncols = self.tasks[i]
                s = self.slots[i % len(self.slots)]
                dst = s.ap.rearrange("p a b -> p (a b)")[:, 0:ncols] if len(s.ap.shape) == 3 else s.ap[:, 0:ncols]
                P.dma("pool", dst, src, writes=[s.b], key=f"{self.keybase}{i % len(self.slots)}", max_dma_last_dim=4096)
                self.issued += 1

            def get(self):
                while self.issued < len(self.tasks) and self.issued < self.used + len(self.slots) - (self.hold - 1):
                    self._issue()
                s = self.slots[self.used % len(self.slots)]
                self.used += 1
                return s

        def rstd_from_ssq(ps, width, dst, reads, n=G):
            ACT(dst.ap[:, 0:n], ps, AF.Ln, reads, [dst.b], scale=1.0 / width, bias=EPS)
            ACT(dst.ap[:, 0:n], dst.ap[:, 0:n], AF.Exp, [dst.b], [dst.b], scale=-0.5)

        def norm_fm_parts(src, nblk, width, gcol, dst_ap, dst_b, sq, rstd, bank):
            def p1():
                ACT(sq.ap[:, 0:nblk, :], src.ap[:, 0:nblk, :], AF.Square, [src.b], [sq.b])

            def p2():
                for j in range(nblk):
                    MM(bank.ap[:, 0:G], ones_bf, sq.ap[:, j, :], j == 0, j == nblk - 1, [sq.b, cbf.b], [bank.b])

            def p3():
                rstd_from_ssq(bank.ap[:, 0:G], width, rstd, [bank.b])

            def p4():
                for j in range(nblk):
                    STT(dst_ap[:, j, :], src.ap[:, j, :], pv[:, gcol + j:gcol + j + 1], rstd.ap[:, 0:G], ALU.mult, ALU.mult,
                        [src.b, rstd.b, pvec.b], [dst_b])
            return [p1, p2, p3, p4]

        def norm_fm(*a):
            for f_ in norm_fm_parts(*a):
                f_()

        def resid_parts(xg, ybuf, gcol, sq, rstd, tmp, bank):
            def p1():
                ACT(sq.ap, ybuf.ap, AF.Square, [ybuf.b], [sq.b])

            def p2():
                for j in range(KC):
                    MM(bank.ap[:, 0:G], ones_bf, sq.ap[:, j, :], j == 0, j == KC - 1, [sq.b, cbf.b], [bank.b])

            def p3():
                rstd_from_ssq(bank.ap[:, 0:G], D, rstd, [bank.b])

            def p4(j0, j1):
                for j in range(j0, j1):
                    t = tmp[j % 2]
                    STT(t.ap, ybuf.ap[:, j, :], pv[:, gcol + j:gcol + j + 1], rstd.ap[:, 0:G], ALU.mult, ALU.mult,
                        [ybuf.b, rstd.b, pvec.b], [t.b])
                    TT("dve", xg.ap[:, j, :], xg.ap[:, j, :], t.ap, ALU.add, [xg.b, t.b], [xg.b])
            return [p1, p2, p3, lambda: p4(0, 4), lambda: p4(4, 8)]

        def resid_update(*a):
            for f_ in resid_parts(*a):
                f_()

        persist_top = A.top
        try:
          for li, l in enumerate(layers):
              gl = l
              if li == 0:
                  Xin = xT_in
              else:
                  Xin = xb
              Xout = yT if li == len(layers) - 1 else xb
              Xmid = xa
              bXin = xin_bufs if li > 0 else [Buf(f"xin{q}") for q in range(NGR)]
              bXmid = [Buf(f"xmid{l}_{q}") for q in range(NGR)]
              bXout = [Buf(f"xout{l}_{q}") for q in range(NGR)]

              P.barrier()
              P.dma("sp", pvec.ap, dr[f"pvec{l}"], writes=[pvec.b], key="constf")
              P.dma("sp", gbias.ap, dr[f"gbias{l}"], writes=[gbias.b], key="constf")
              P.dma("pool", poolw.ap.rearrange("p a b -> p (a b)"), dr[f"poolw{l}"], writes=[poolw.b], key="const")
              MS("pool", hal_m.ap, 0.0, [hal_m.b])
              MS("pool", hal_f.ap, 0.0, hal_f_b)
              MS("pool", Cst.ap, 0.0, [Cst.b])
              MS("pool", Cbf.ap, 0.0, [Cbf.b])

              A.top = persist_top
              qk_m = new(BF16, [8, G], "qk_m")
              qzs = [new(BF16, [4, G], f"qz{i}") for i in range(2)]
              up = new(F32, [2, 16 + G], "up")
              v_aug = new(BF16, [4, 4, 130], "v_aug")
              sig_o = new(BF16, [4, 512], "sig_o")
              gpre = new(F32, [4, 8], "gpre")
              mixTs = [new(BF16, [8, G], f"mixT{i}") for i in range(2)]
              xg = new(F32, [8, G], "xg")
              sq = new(BF16, [8, G], "sq")
              rstd = new(F32, [G], "rstd")
              hT = new(BF16, [8, G], "hT")
              wfm = [new(BF16, [8, 128], f"wfm{i}") for i in range(3)]
              wtm = [new(BF16, [8, 264], f"wtm{i}") for i in range(2)]
              raw = [new(F32, [G + 4], f"raw{i}") for i in range(2)]
              acc = [new(F32, [G], f"acc{i}") for i in range(2)]
              for t_ in raw:
                  t_.h = Buf("rawh")
              xg3 = xg
              ybuf = new(F32, [8, G], "ybuf")
              tmp = [new(F32, [G], f"tmp{i}") for i in range(2)]
              gsp = new(F32, [4, 4], "gsp")
              gu = new(F32, [4, 4], "gu")
              gw = new(F32, [4, 4], "gw")
              ge = new(F32, [4, 4], "ge")
              mST = [new(BF16, [128], f"mST{i}") for i in range(2)]
              ktok = [new(BF16, [128], f"ktok{i}") for i in range(2)]
              vp = [new(BF16, [130], f"vp{i}") for i in range(2)]
              dn = [new(F32, [4], f"dn{i}") for i in range(2)]
              Ct = [new(F32, [132], f"Ct{i}") for i in range(2)]
              ym = [new(F32, [512], f"ym{i}") for i in range(2)]
              ymn = [new(BF16, [512], f"ymn{i}") for i in range(2)]
              junk = new(BF16, [512], "junk")
              ss = [new(F32, [2], f"ss{i}") for i in range(2)]
              ebuf = [new(F32, [G], f"e{i}") for i in range(2)]
              spb = [new(BF16, [G], f"sp{i}") for i in range(3)]
              aT = [new(BF16, [G], f"aT{i}") for i in range(3)]
              accr = [[new(BF16, [G], f"accr{h_}_{i}") for i in range(3)] for h_ in range(2)]
              ysb = new(F32, [2, G], "ysb")
              sq2 = new(BF16, [2, G], "sq2")
              rstd2 = new(F32, [G], "rstd2")
              s_a = new(F32, [16 + G], "s_a")
              s_b = new(F32, [16 + G], "s_b")
              ypT = new(BF16, [2, G], "ypT")
              ypo = new(F32, [2, G], "ypo")
              ksb_b = [Buf(f"ksb{i}") for i in range(NGR)]
              vsb_b = [Buf(f"vsb{i}") for i in range(NGR)]

              def sub(bank, ap, name):
                  t_ = TB(ap, name)
                  t_.b = bank.b
                  return t_
              pG = sub(pb[7], pb[7].ap[:, 448:480], "pG")
              pST = sub(pb[6], pb[6].ap[:, 0:128], "pST")
              pKT = sub(pb[6], pb[6].ap.bitcast(BF16)[:, 256:384], "pKT")
              pTt = sub(pb[6], pb[6].ap.bitcast(BF16)[:, 512:1024], "pTt")
              pH = sub(pb[7], pb[7].ap[:, 0:129], "pH")
              pC = sub(pb[7], pb[7].ap[:, 256:385], "pC")
              wtm_d = dr[f"wtm{l}"].rearrange("p (k c) -> p k c", k=KC)
              tmcols = [(0, 256, "v", 0), (256, 256, "v", 1), (512, 256, "o", 0), (768, 256, "o", 1), (1024, 264, "s", 0)]
              sbank = [pb[4], pb[5]]

              def s1_steps(g):
                  T0 = g * G
                  qz = qzs[g % 2]
                  steps = []
                  ws = WStream(wfm, "wfm")
                  for bi in range(14):
                      ws.add(dr[f"wfm{l}"][bi], KC * 128)
                  state = {"wt_issued": 0, "nb": 0}

                  def st_norm():
                      P.dma("sp", xg.ap, xview(Xin, g), reads=[bXin[g]], writes=[xg.b], key="ldx")
                      MS("pool", qz.ap, 0.0, [qz.b])
                  steps.append(st_norm)
                  steps.extend(norm_fm_parts(xg, KC, D, PV_PREMIX, hT.ap, hT.b, sq, rstd, pb[4]))

                  def st_fm_taps(bi):
                      rb = raw[bi % 2]
                      ab = acc[bi % 2]
                      cw = PV_CWM + bi * 4
                      TS("dve", ab.ap, rb.ap[:, 3:3 + G], pv[:, cw + 3:cw + 4], pv[:, PV_CBM + bi:PV_CBM + bi + 1],
                         ALU.mult, ALU.add, [rb.b, pvec.b], [ab.b])
                      for j in (2, 1, 0):
                          STT(ab.ap, rb.ap[:, j:j + G], pv[:, cw + j:cw + j + 1], ab.ap, ALU.mult, ALU.add,
                              [rb.b, rb.h, ab.b, pvec.b], [ab.b])

                  def st_fm_silu(bi):
                      ACT(qk_m.ap[:, bi, :], acc[bi % 2].ap, AF.Silu, [acc[bi % 2].b], [qk_m.b])

                  def st_fm(bi):
                      wsl = ws.get()
                      bank = sbank[bi % 2]
                      for k in range(KC):
                          MM(bank.ap[:, 0:G], wsl.ap[:, k, :], hT.ap[:, k, :], k == 0, k == KC - 1, [wsl.b, hT.b], [bank.b])
                      if bi < 8:
                          rb = raw[bi % 2]
                          ACT(rb.ap[:, 0:3], hal_m.ap[:, bi, 0:3], AF.Copy, [hal_m.b], [rb.h])
                          ACT(rb.ap[:, 3:3 + G], bank.ap[:, 0:G], AF.Copy, [bank.b], [rb.b])
                          ACT(hal_m.ap[:, bi, 0:3], rb.ap[:, G:G + 3], AF.Copy, [rb.b], [hal_m.b])
                      elif bi < 10:
                          b = bi - 8
                          ACT(qz.ap[0:64, 2 * b, :], bank.ap[0:64, 0:G], AF.Copy, [bank.b], [qz.b], scale=0.125)
                          ACT(qz.ap[64:128, 2 * b + 1, :], bank.ap[64:128, 0:G], AF.Copy, [bank.b], [qz.b], scale=0.125)
                      elif bi < 12:
                          b = bi - 10
                          ACT(k_sb.ap[:, b, T0:T0 + G], bank.ap[:, 0:G], AF.Copy, [bank.b], [ksb_b[g]])
                      else:
                          b = bi - 12
                          if b == 0:
                              if g == 0:
                                  MS("pool", up.ap[:, :, 0:16], 0.0, [up.b])
                              else:
                                  CP("pool", up.ap[:, :, 0:16], hal_p.ap, [hal_p.b], [up.b])
                          ACT(up.ap[:, b, 16:16 + G], bank.ap[:, 0:G], AF.Copy, [bank.b], [up.b])
                  for bi in range(14):
                      steps.append((lambda bi_: (lambda: st_fm(bi_)))(bi))
                      if bi < 8:
                          steps.append((lambda bi_: (lambda: st_fm_taps(bi_)))(bi))
                          steps.append((lambda bi_: (lambda: st_fm_silu(bi_)))(bi))

                  def wt_get(ci):
                      while state["wt_issued"] < len(tmcols) and state["wt_issued"] <= ci + 1:
                          i_ = state["wt_issued"]
                          c0_, nc_, _, _ = tmcols[i_]
                          s_ = wtm[i_ % 2]
                          P.dma("pool", s_.ap[:, :, 0:nc_], wtm_d[:, :, c0_:c0_ + nc_], writes=[s_.b], key=f"wtm{i_ % 2}",
                                max_dma_last_dim=4096)
                          state["wt_issued"] += 1
                      return wtm[ci % 2]

                  def st_tm(ci, i):
                      c0, ncol, kind, half = tmcols[ci]
                      if i == 0:
                          if ci == 0:
                              MS("pool", v_aug.ap, 1.0, [v_aug.b])
                      wsl = wt_get(ci) if i == 0 else wtm[ci % 2]
                      bank = sbank[state["nb"] % 2]
                      state["nb"] += 1
                      for k in range(KC):
                          MM(bank.ap[:, 0:ncol], hT.ap[:, k, i * 128:(i + 1) * 128], wsl.ap[:, k, 0:ncol], k == 0, k == KC - 1,
                             [wsl.b, hT.b], [bank.b])
                      if kind == "v":
                          ACT(v_aug.ap[:, i, 2 * half:2 * half + 2, 0:128], bank.ap[:, 0:256].rearrange("p (h d) -> p h d", h=2), AF.Copy,
                              [bank.b], [v_aug.b])
                      elif kind == "o":
                          ACT(sig_o.ap[:, i, 256 * half:256 * half + 256], bank.ap[:, 0:256], AF.Sigmoid, [bank.b], [sig_o.b])
                      else:
                          ACT(v_sb.ap[:, T0 // 128 + i, :], bank.ap[:, 0:256], AF.Copy, [bank.b], [vsb_b[g]])
                          TT("dve", gpre.ap[:, i, :], bank.ap[:, 256:264], gbias.ap[:, 0:8], ALU.add, [bank.b, gbias.b], [gpre.b])
                  for ci in range(len(tmcols)):
                      for i in range(4):
                          steps.append((lambda ci_, i_: (lambda: st_tm(ci_, i_)))(ci, i))
                  return steps

              def s3_steps(g):
                  mixT = mixTs[g % 2]
                  steps = []
                  ws = WStream(wfm, "wfm")
                  for m in range(8):
                      ws.add(dr[f"wout{l}"][m], KC * 128)

                  def st_ld():
                      P.dma("sp", xg3.ap, xview(Xin, g), reads=[bXin[g]], writes=[xg3.b], key="ldx")
                  steps.append(st_ld)

                  def st_m(m):
                      wsl = ws.get()
                      bank = sbank[m % 2]
                      for k in range(KC):
                          MM(bank.ap[:, 0:G], wsl.ap[:, k, :], mixT.ap[:, k, :], k == 0, k == KC - 1, [wsl.b, mixT.b], [bank.b])
                      ACT(ybuf.ap[:, m, :], bank.ap[:, 0:G], AF.Copy, [bank.b], [ybuf.b])
                  for m in range(8):
                      steps.append((lambda m_: (lambda: st_m(m_)))(m))

                  steps.extend(resid_parts(xg3, ybuf, PV_POSTMIX, sq, rstd, tmp, pb[4]))

                  def st_res():
                      P.dma("sp", xview(Xmid, g), xg3.ap, reads=[xg3.b], writes=[bXmid[g]], key="st")
                      if debug and li == 0:
                          P.dma("sp", xview(dbg_mid, g), xg3.ap, reads=[xg3.b], writes=[Buf("dbgx")], key="st")
                  steps.append(st_res)
                  return steps

              def gates(g):
                  gi = gpre.ap[:, :, 0:4]
                  gf = gpre.ap[:, :, 4:8]
                  ACT(gsp.ap, gf, AF.Exp, [gpre.b], [gsp.b], scale=-1.0)
                  ACT(gsp.ap, gsp.ap, AF.Ln, [gsp.b], [gsp.b], bias=1.0)
                  gsp2 = gsp.ap.rearrange("p a b -> p (a b)")
                  MM(pG.ap[:, 0:16], tri_f, gsp2, True, True, [gsp.b, cf32.b], [pG.b])
                  MM(pG.ap[:, 16:32], ones_f, gsp2, True, True, [gsp.b, cf32.b], [pG.b])
                  cum = pG.ap[:, 0:16].rearrange("p (a b) -> p a b", a=4)
                  tot = pG.ap[:, 16:32].rearrange("p (a b) -> p a b", a=4)
                  TT("dve", gu.ap, gi, cum, ALU.add, [gpre.b, pG.b], [gu.b])
                  ACT(gu.ap, gu.ap, AF.Exp, [gu.b], [gu.b], bias=math.log(128.0 ** -0.5))
                  ACT(gw.ap, cum, AF.Exp, [pG.b], [gw.b], scale=-1.0)
                  ACT(ge.ap, tot, AF.Exp, [pG.b], [ge.b], scale=-1.0)

              def ml_steps_for(g):
                  mixT = mixTs[g % 2]

                  def ml_u1(i, h, sl):
                      ts_ = slice(i * 128, (i + 1) * 128)
                      qT = qk_m.ap[:, h, ts_]
                      kT = qk_m.ap[:, 4 + h, ts_]
                      col = slice(h, h + 1)
                      MM(pST.ap, kT, qT, True, True, [qk_m.b], [pST.b])
                      TS("dve", vp[sl].ap[:, 0:129], v_aug.ap[:, i, h, 0:129], gu.ap[:, i, col], None, ALU.mult, None,
                         [v_aug.b, gu.b], [vp[sl].b])

                  def ml_u2(i, h, sl):
                      ts_ = slice(i * 128, (i + 1) * 128)
                      qT = qk_m.ap[:, h, ts_]
                      kT = qk_m.ap[:, 4 + h, ts_]
                      TT("dve", mST[sl].ap, pST.ap, mask_ml, ALU.mult, [pST.b, cbf.b], [mST[sl].b])
                      TR(pKT.ap, kT, [qk_m.b], [pKT.b])
                      MM(pH.ap, mST[sl].ap, vp[sl].ap[:, 0:129], True, False, [mST[sl].b, vp[sl].b], [pH.b])
                      MM(pH.ap, qT, Cbf.ap[:, h, 0:129], False, True, [qk_m.b, Cbf.b], [pH.b])
                      ACT(ktok[sl].ap, pKT.ap, AF.Copy, [pKT.b], [ktok[sl].b])

                  def ml_u3(i, h, sl):
                      ymt = ym[i % 2]
                      col = slice(h, h + 1)
                      d = dn[sl]
                      ACT(d.ap[:, 0:1], pH.ap[:, 128:129], AF.Abs, [pH.b, gw.b], [d.b], scale=gw.ap[:, i, col])
                      TS("dve", d.ap[:, 1:2], d.ap[:, 0:1], 1.0, None, ALU.max, None, [d.b], [d.b])
                      P.op("dve", (lambda dd: lambda e: e.reciprocal(out=dd.ap[:, 2:3], in_=dd.ap[:, 1:2]))(d), [d.b], [d.b])
                      TT("dve", d.ap[:, 3:4], d.ap[:, 2:3], gw.ap[:, i, col], ALU.mult, [d.b, gw.b], [d.b])
                      STT(ymt.ap[:, h * 128:(h + 1) * 128], pH.ap[:, 0:128], d.ap[:, 3:4], sig_o.ap[:, i, h * 128:(h + 1) * 128],
                          ALU.mult, ALU.mult, [pH.b, d.b, sig_o.b], [ymt.b])
                      MM(pC.ap, ktok[sl].ap, vp[sl].ap[:, 0:129], True, True, [ktok[sl].b, vp[sl].b], [pC.b])

                  def ml_u4(i, h, sl):
                      col = slice(h, h + 1)
                      TT("dve", Ct[sl].ap[:, 0:129], Cst.ap[:, h, 0:129], pC.ap, ALU.add, [Cst.b, pC.b], [Ct[sl].b])
                      ACT(Cst.ap[:, h, 0:129], Ct[sl].ap[:, 0:129], AF.Copy, [Ct[sl].b, ge.b], [Cst.b], scale=ge.ap[:, i, col])
                      ACT(Cbf.ap[:, h, 0:129], Ct[sl].ap[:, 0:129], AF.Copy, [Ct[sl].b, ge.b], [Cbf.b], scale=ge.ap[:, i, col])

                  def ml_fin(i):
                      ts_ = slice(i * 128, (i + 1) * 128)
                      ymt = ym[i % 2]
                      s1 = ss[i % 2]
                      ACT(junk.ap, ymt.ap, AF.Square, [ymt.b], [junk.b, s1.b], accum=s1.ap[:, 0:1])
                      ACT(s1.ap[:, 1:2], s1.ap[:, 0:1], AF.Ln, [s1.b], [s1.b], scale=1.0 / 512, bias=EPS)
                      ACT(s1.ap[:, 1:2], s1.ap[:, 1:2], AF.Exp, [s1.b], [s1.b], scale=-0.5)
                      ACT(ymn[i % 2].ap, ymt.ap, AF.Copy, [ymt.b, s1.b], [ymn[i % 2].b], scale=s1.ap[:, 1:2])
                      for h in range(4):
                          TR(pTt.ap[:, h * 128:(h + 1) * 128], ymn[i % 2].ap[:, h * 128:(h + 1) * 128], [ymn[i % 2].b], [pTt.b])
                      for h in range(4):
                          ACT(mixT.ap[:, h, ts_], pTt.ap[:, h * 128:(h + 1) * 128], AF.Copy, [pTt.b, pvec.b], [mixT.b],
                              scale=pv[:, PV_MIXG + h:PV_MIXG + h + 1])

                  steps = []
                  cnt = 0
                  for i in range(4):
                      for h in range(4):
                          for fn__ in (ml_u1, ml_u2, ml_u3, ml_u4):
                              steps.append((lambda f_, a_: (lambda: f_(*a_)))(fn__, (i, h, cnt % 2)))
                          cnt += 1
                      steps.append((lambda i_: (lambda: ml_fin(i_)))(i))
                  return steps

              def pool_step(g):
                  mixT = mixTs[g % 2]
                  for b in range(2):
                      ub = up.ap[:, b, :]
                      W_ = 16 + G
                      TT("pool", s_a.ap[:, 1:W_], ub[:, 1:W_], ub[:, 0:W_ - 1], ALU.add, [up.b], [s_a.b])
                      if b == 0:
                          TT("pool", s_b.ap[64:128, 3:W_], s_a.ap[64:128, 3:W_], s_a.ap[64:128, 1:W_ - 2], ALU.add, [s_a.b], [s_b.b])
                      else:
                          TT("pool", s_b.ap[:, 3:W_], s_a.ap[:, 3:W_], s_a.ap[:, 1:W_ - 2], ALU.add, [s_a.b], [s_b.b])
                          TT("pool", s_a.ap[:, 7:W_], s_b.ap[:, 7:W_], s_b.ap[:, 3:W_ - 4], ALU.add, [s_b.b], [s_a.b])
                          TT("pool", s_b.ap[64:128, 15:W_], s_a.ap[64:128, 15:W_], s_a.ap[64:128, 7:W_ - 8], ALU.add, [s_a.b], [s_b.b])
                      for (src, ps_) in ((s_a, slice(0, 64)), (s_b, slice(64, 128))):
                          if g == 0:
                              TT("dve", src.ap[ps_, 16:32], src.ap[ps_, 16:32], poolcorr[ps_, b, :], ALU.mult, [src.b, cf32.b], [src.b])
                          STT(ypT.ap[ps_, b, :], src.ap[ps_, 16:W_], pv[ps_, PV_INVW + b:PV_INVW + b + 1], ub[ps_, 16:W_],
                              ALU.mult, ALU.subtract, [src.b, up.b, pvec.b], [ypT.b])
                      bank = pb[6 + b]
                      MM(bank.ap[:, 0:G], poolw.ap[:, b, :], ypT.ap[:, b, :], True, True, [poolw.b, ypT.b], [bank.b])
                      ACT(ypo.ap[:, b, :], bank.ap[:, 0:G], AF.Copy, [bank.b, pvec.b], [ypo.b], scale=pv[:, PV_PSCALE + b:PV_PSCALE + b + 1])
                  CP("pool", hal_p.ap, up.ap[:, :, G:G + 16], [up.b], [hal_p.b])
                  norm_fm(ypo, 2, 256, PV_MIXG + 6, mixT.ap[:, 6:8, :], mixT.b, sq2, rstd2, pb[7])

              def sb_loop(g):
                  mixT = mixTs[g % 2]
                  qz = qzs[g % 2]
                  nj = 4 * g + 4
                  pairs = []
                  for b in range(2):
                      for hh in range(2):
                          for idx, j in enumerate(reversed(range(nj))):
                              pairs.append((b, hh, j, idx))

                  def sb_geom(p):
                      b, hh, j, idx = pairs[p]
                      r = j - 4 * g
                      tl = 128 * r if r >= 0 else 0
                      return b, hh, j, idx, r, tl, G - tl

                  def stageA(p):
                      b, hh, j, idx, r, tl, N = sb_geom(p)
                      hd = 2 * b + hh
                      zs, s3 = p % 2, p % 3
                      ring = accr[hh]
                      if idx == 0:
                          for t_ in ring:
                              MS("pool", t_.ap, 0.0, [t_.b])
                      pZ = pb[0]
                      kTj = k_sb.ap[:, b, j * 128:(j + 1) * 128]
                      qh = qz.ap[:, hd, tl:G]
                      MM(pZ.ap[:, 0:N], kTj, qh, True, True, [ksb_b[j // 4], qz.b], [pZ.b])
                      ACT(ebuf[zs].ap[:, 0:N], pZ.ap[:, 0:N], AF.Exp, [pZ.b], [ebuf[zs].b])
                      ACT(spb[s3].ap[:, 0:N], ebuf[zs].ap[:, 0:N], AF.Ln, [ebuf[zs].b], [spb[s3].b], bias=1.0)
                      if r >= 0:
                          TT("dve", spb[s3].ap[:, 0:N], spb[s3].ap[:, 0:N], sbmask[:, 0:N], ALU.mult, [spb[s3].b, cbf.b], [spb[s3].b])
                      if j > 0:
                          src, dst = ring[idx % 3], ring[(idx + 1) % 3]
                          TT("dve", dst.ap[:, tl:G], src.ap[:, tl:G], spb[s3].ap[:, 0:N], ALU.add, [src.b, spb[s3].b], [dst.b])

                  def stageB(p):
                      b, hh, j, idx, r, tl, N = sb_geom(p)
                      hd = 2 * b + hh
                      zs, s3 = p % 2, p % 3
                      ring = accr[hh]
                      first = idx == 0
                      pZR = pb[1 + zs]
                      kTj = k_sb.ap[:, b, j * 128:(j + 1) * 128]
                      qh = qz.ap[:, hd, tl:G]
                      MM(pZR.ap[:, 0:N], kTj, qh, True, False, [ksb_b[j // 4], qz.b], [pZR.b])
                      MM(pZR.ap[:, 0:N], tri_neg, spb[s3].ap[:, 0:N], False, first, [spb[s3].b, cbf.b], [pZR.b])
                      if not first:
                          cur = ring[idx % 3]
                          MM(pZR.ap[:, 0:N], ones_neg, cur.ap[:, tl:G], False, True, [cur.b, cbf.b], [pZR.b])
                      ACT(aT[s3].ap[:, 0:N], pZR.ap[:, 0:N], AF.Exp, [pZR.b], [aT[s3].b])
                      if r >= 0:
                          TT("dve", aT[s3].ap[:, 0:N], aT[s3].ap[:, 0:N], sbmask[:, 0:N], ALU.mult, [aT[s3].b, cbf.b], [aT[s3].b])

                  def stageC(p):
                      b, hh, j, idx, r, tl, N = sb_geom(p)
                      s3 = p % 3
                      first = idx == 0
                      pO = pb[3]
                      MM(pO.ap[:, tl:G], v_sb.ap[:, j, b * 128:(b + 1) * 128], aT[s3].ap[:, 0:N], first, j == 0,
                         [vsb_b[j // 4], aT[s3].b], [pO.b], sgc=True)
                      if j == 0:
                          ps_ = slice(64 * hh, 64 * hh + 64)
                          ACT(ysb.ap[ps_, b, :], pO.ap[ps_, 0:G], AF.Copy, [pO.b], [ysb.b])

                  stageA(0)
                  for p in range(len(pairs)):
                      if p + 1 < len(pairs):
                          stageA(p + 1)
                      stageB(p)
                      if p >= 1:
                          stageC(p - 1)
                  stageC(len(pairs) - 1)

              def run_all(fns):
                  for f_ in fns:
                      f_()
              for st_ in s1_steps(0):
                  st_()
              for g in range(NGR):
                  gates(g)
                  S_sb = P.record(lambda: sb_loop(g))
                  S_a1 = P.record(lambda: run_all(s3_steps(g - 1))) if g > 0 else []
                  S_b = P.record(lambda: (pool_step(g), run_all(ml_steps_for(g))))
                  S_a2 = P.record(lambda: run_all(s1_steps(g + 1))) if g + 1 < NGR else []
                  P.run_streams([S_sb, S_a1, S_b, S_a2], after={3: (1, 2)})
                  mixT = mixTs[g % 2]
                  norm_fm(ysb, 2, 256, PV_MIXG + 4, mixT.ap[:, 4:6, :], mixT.b, sq2, rstd2, pb[0])
                  chk(f"S3_{g}")
              for st_ in s3_steps(NGR - 1):
                  st_()
              P.barrier()
              chk("mixer")
              for hf in HF_RANGE:
                  A.top = persist_top
                  hT2 = new(BF16, [8, 2 * G], "hT2")
                  actT = new(BF16, [NF, 2 * G], "actT")
                  xgs = [new(F32, [8, G], f"xgf{i}") for i in range(2)]
                  sqs = [new(BF16, [8, G], f"sqf{i}") for i in range(2)]
                  rstds = [new(F32, [G], f"rstdf{i}") for i in range(2)]
                  markf = A.top
                  wfm = [new(BF16, [8, 128], f"wu{i}") for i in range(4)]
                  NSL = 4
                  rawg = [new(F32, [G + 4], f"rawg{i}") for i in range(NSL)]
                  rawv = [new(F32, [G + 4], f"rawv{i}") for i in range(NSL)]
                  accg = [new(F32, [G], f"accg{i}") for i in range(NSL)]
                  accv = [new(F32, [G], f"accv{i}") for i in range(NSL)]
                  gel = [new(F32, [G], f"gel{i}") for i in range(NSL)]
                  for t_ in rawg + rawv:
                      t_.h = Buf("rwh")
                  for qq in range(2):
                      Q = 2 * hf + qq
                      P.dma("sp", xgs[qq].ap, xview(Xmid, Q), reads=[bXmid[Q]], writes=[xgs[qq].b], key=f"ldxf{qq}")
                  for qq in range(2):
                      norm_fm(xgs[qq], KC, D, PV_PREFFN, hT2.ap[:, :, qq * G:(qq + 1) * G], hT2.b, sqs[qq], rstds[qq], pb[6 + qq])
                  chk("ffn_norm")
                  ws = WStream(wfm, "wfm", hold=2)
                  for f in range(NF):
                      ws.add(dr[f"wup{l}"][f], KC * 128)
                      ws.add(dr[f"wup{l}"][NF + f], KC * 128)
                  items = []

                  def ffn_front(f, qq, wg, wv, n):
                      Q = 2 * hf + qq
                      sl = n % NSL
                      bsl = n % 4
                      pg, pvb = pb[2 * bsl], pb[2 * bsl + 1]
                      cs = slice(qq * G, (qq + 1) * G)
                      for k in range(KC):
                          MM(pg.ap[:, 0:G], wg.ap[:, k, :], hT2.ap[:, k, cs], k == 0, k == KC - 1, [wg.b, hT2.b], [pg.b])
                      for k in range(KC):
                          MM(pvb.ap[:, 0:G], wv.ap[:, k, :], hT2.ap[:, k, cs], k == 0, k == KC - 1, [wv.b, hT2.b], [pvb.b])
                      for (fb, bank, rw, ac) in ((f, pg, rawg[sl], accg[sl]), (NF + f, pvb, rawv[sl], accv[sl])):
                          cw = PV_CWF + fb * 3
                          hb = hal_f_b[fb % 4]
                          CP("pool", rw.ap[:, 0:2], hal_f.ap[:, fb, :], [hb], [rw.h])
                          ACT(rw.ap[:, 2:2 + G], bank.ap[:, 0:G], AF.Copy, [bank.b], [rw.b])
                          CP("pool", hal_f.ap[:, fb, :], rw.ap[:, G:G + 2], [rw.b], [hb])
                          ACT(ac.ap, bank.ap[:, 0:G], AF.Identity, [bank.b, pvec.b], [ac.b], scale=pv[:, cw + 2:cw + 3],
                              bias=pv[:, PV_CBF + fb:PV_CBF + fb + 1])
                          for j in (1, 0):
                              STT(ac.ap, rw.ap[:, j:j + G], pv[:, cw + j:cw + j + 1], ac.ap, ALU.mult, ALU.add,
                                  [rw.b, rw.h, ac.b, pvec.b], [ac.b])

                  def ffn_back(f, qq, n):
                      sl = n % NSL
                      cs = slice(qq * G, (qq + 1) * G)
                      ACT(gel[sl].ap, accg[sl].ap, AF.Gelu_apprx_tanh, [accg[sl].b], [gel[sl].b])
                      TT("dve", actT.ap[:, f, cs], gel[sl].ap, accv[sl].ap, ALU.mult, [gel[sl].b, accv[sl].b], [actT.b])

                  cnt = 0
                  prev = None
                  for f in range(NF):
                      wg = ws.get()
                      wv = ws.get()
                      for qq in range(2):
                          ffn_front(f, qq, wg, wv, cnt)
                          if prev is not None:
                              ffn_back(*prev)
                          prev = (f, qq, cnt)
                          cnt += 1
                      chk(f"ffn_f{f}")
                  ffn_back(*prev)
                  if debug and li == 0:
                      P.dma("pool", dbg_act.rearrange("(k p) t -> p k t", p=128)[:, :, hf * 2 * G:(hf + 1) * 2 * G], actT.ap,
                            reads=[actT.b], writes=[Buf("dbga")], key="st")
                      P.dma("pool", dbg_h2.rearrange("(k p) t -> p k t", p=128)[:, :, hf * 2 * G:(hf + 1) * 2 * G], hT2.ap,
                            reads=[hT2.b], writes=[Buf("dbgh")], key="st")
                  chk("ffn_up")
                  P.barrier()
                  A.top = markf
                  ybuf2 = [new(F32, [8, G], f"ybuff{i}") for i in range(2)]
                  tmps = [[new(F32, [G], f"tmpf{q_}_{i}") for i in range(2)] for q_ in range(2)]
                  wdn = [new(BF16, [NF, 128], f"wd{i}") for i in range(2)]
                  wd = WStream(wdn, "wdn")
                  for qq in range(2):
                      for m in range(8):
                          wd.add(dr[f"wdn{l}"][m], NF * 128)
                  nb_ = 0
                  for qq in range(2):
                      Q = 2 * hf + qq
                      cs = slice(qq * G, (qq + 1) * G)
                      for m in range(8):
                          wsl = wd.get()
                          bank = pb[nb_ % 4]
                          nb_ += 1
                          for k in range(NF):
                              MM(bank.ap[:, 0:G], wsl.ap[:, k, :], actT.ap[:, k, cs], k == 0, k == NF - 1, [wsl.b, actT.b], [bank.b])
                          ACT(ybuf2[qq].ap[:, m, :], bank.ap[:, 0:G], AF.Copy, [bank.b], [ybuf2[qq].b])
                      if qq == 1:
                          chk("ffn_dn")
                      resid_update(xgs[qq], ybuf2[qq], PV_POSTFFN, sqs[qq], rstds[qq], tmps[qq], pb[6 + qq])
                      P.dma("sp", xview(Xout, Q), xgs[qq].ap, reads=[xgs[qq].b], writes=[bXout[Q]], key="st")
                      chk(f"ffn_q{Q}")
                  P.barrier()
              xin_bufs = bXout

        except StopBuild as ex:
            print("[kernel] build stopped at", ex)
        nw, ni = P.emit(final_wait_keys=[k for k in ["st"] if k in P.dma_count])
        print(f"[kernel] instrs={ni} waits={nw} arena_peak_words={A.peak}")
    return nc


def _consts():
    s = np.arange(128)[:, None]
    t = np.arange(128)[None, :]
    ident = np.eye(128, dtype=np.float32)
    tri_neg = -(s >= t).astype(np.float32)
    ones_neg = -np.ones((128, 128), np.float32)
    ones = np.ones((128, 128), np.float32)
    mask_ml = (s <= t).astype(np.float32)
    c = np.arange(512)[None, :]
    sbmask = (c > s).astype(np.float32)
    cbf = np.concatenate([ident, tri_neg, ones_neg, ones, mask_ml, sbmask], axis=1)
    wins = np.array([2, 4, 8, 16], np.float32)
    corr = np.zeros((128, 2, 16), np.float32)
    for b in range(2):
        for half in range(2):
            w = wins[2 * b + half]
            tt = np.arange(16, dtype=np.float32)
            corr[half * 64:(half + 1) * 64, b, :] = w / np.minimum(tt + 1.0, w)
    cf32 = np.concatenate([mask_ml, ones, corr.reshape(128, 32)], axis=1)
    return np.ascontiguousarray(cbf), np.ascontiguousarray(cf32)


def _fm(v):
    return np.ascontiguousarray(v.reshape(-1, 128).T)


def _blk(w, c0, nblk):
    K = w.shape[0]
    kc = K // 128
    sub = w[:, c0:c0 + nblk * 128].reshape(kc, 128, nblk, 128)
    return np.ascontiguousarray(sub.transpose(2, 1, 0, 3).reshape(nblk, 128, kc * 128))


def _prep_layer(inp, l):
    w_in = inp["w_in"][l]
    fm = np.concatenate([_blk(w_in, 0, 4), _blk(w_in, 512, 4), _blk(w_in, 2056, 2), _blk(w_in, 2312, 2), _blk(w_in, 2824, 2)], axis=0)
    tmc = np.concatenate([w_in[:, 1024:1536], w_in[:, 1536:2048], w_in[:, 2568:2824], w_in[:, 2048:2056]], axis=1)
    tm = np.ascontiguousarray(tmc.reshape(KC, 128, 1288).transpose(1, 0, 2).reshape(128, KC * 1288))
    wout = _blk(inp["w_out"][l], 0, 8)
    wup = _blk(inp["ffn_w_up"][l], 0, 2 * NF)
    wdn = _blk(inp["ffn_w_down"][l], 0, 8)
    pvec = np.zeros((128, NPV), np.float32)
    pvec[:, PV_PREMIX:PV_PREMIX + 8] = _fm(inp["pre_mix_g"][l])
    pvec[:, PV_POSTMIX:PV_POSTMIX + 8] = _fm(inp["post_mix_g"][l])
    pvec[:, PV_PREFFN:PV_PREFFN + 8] = _fm(inp["pre_ffn_g"][l])
    pvec[:, PV_POSTFFN:PV_POSTFFN + 8] = _fm(inp["post_ffn_g"][l])
    pvec[:, PV_MIXG:PV_MIXG + 8] = _fm(np.concatenate([inp["mlstm_out_g"][l], inp["sb_out_g"][l], inp["pool_out_g"][l]]))
    cwm = inp["mlstm_conv_w"][l]
    pvec[:, PV_CWM:PV_CWM + 32] = cwm.reshape(4, 8, 128).transpose(2, 1, 0).reshape(128, 32)
    pvec[:, PV_CBM:PV_CBM + 8] = _fm(inp["mlstm_conv_b"][l])
    cwf = inp["ffn_conv_w"][l]
    pvec[:, PV_CWF:PV_CWF + 132] = cwf.reshape(3, 44, 128).transpose(2, 1, 0).reshape(128, 132)
    pvec[:, PV_CBF:PV_CBF + 44] = _fm(inp["ffn_conv_b"][l])
    pvec[:, PV_PSCALE:PV_PSCALE + 2] = _fm(inp["pool_scale"][l])
    invw = np.zeros((128, 2), np.float32)
    invw[0:64, 0], invw[64:128, 0], invw[0:64, 1], invw[64:128, 1] = 0.5, 0.25, 0.125, 0.0625
    pvec[:, PV_INVW:PV_INVW + 2] = invw
    gb = np.concatenate([inp["i_bias"][l], inp["f_bias"][l]]).astype(np.float32)
    gbias = np.ascontiguousarray(np.broadcast_to(np.tile(gb, 4)[None, :], (128, 32)))
    pw = inp["pool_w"][l]
    poolw = np.zeros((128, 2, 128), np.float32)
    for b in range(2):
        poolw[0:64, b, 0:64] = pw[2 * b]
        poolw[64:128, b, 64:128] = pw[2 * b + 1]
    return {f"wfm{l}": fm, f"wtm{l}": tm, f"wout{l}": wout, f"wup{l}": wup, f"wdn{l}": wdn,
            f"pvec{l}": pvec, f"gbias{l}": gbias, f"poolw{l}": np.ascontiguousarray(poolw.reshape(128, 256))}


FUSED = True


def kernel(**inputs):
    inp = {k: np.asarray(v, dtype=np.float32) for k, v in inputs.items()}
    x = inp["x"]
    cbf, cf32 = _consts()
    lay = [_prep_layer(inp, l) for l in range(2)]
    xT = [np.ascontiguousarray(x[b].T) for b in range(NCORES)]
    if FUSED:
        nc = build_program([0, 1])
        maps = []
        for b in range(NCORES):
            m = {"xT": xT[b], "cbf": cbf, "cf32": cf32}
            m.update(lay[0])
            m.update(lay[1])
            maps.append(m)
        res = run_bass_kernel_spmd(nc, maps, core_ids=list(range(NCORES)))
        yT = [res.results[b]["yT"] for b in range(NCORES)]
    else:
        cur = xT
        for l in range(2):
            nc = build_program([l])
            maps = []
            for b in range(NCORES):
                m = {"xT": cur[b], "cbf": cbf, "cf32": cf32}
                m.update(lay[l])
                maps.append(m)
            res = run_bass_kernel_spmd(nc, maps, core_ids=list(range(NCORES)))
            cur = [np.ascontiguousarray(res.results[b]["yT"]) for b in range(NCORES)]
        yT = cur
    out = np.stack([np.asarray(yT[b]).T for b in range(NCORES)], axis=0)
    return np.ascontiguousarray(out.astype(np.float32))
```

```python
import math
import numpy as np
from contextlib import ExitStack
import concourse.bass as bass
import concourse.mybir as mybir
from concourse.bass_utils import run_bass_kernel_spmd

F32 = mybir.dt.float32
BF16 = mybir.dt.bfloat16
AF = mybir.ActivationFunctionType
ALU = mybir.AluOpType

S = 2048
D = 1024
KC = 8
G = 512
NGR = S // G
DFF = 2816
NF = DFF // 128
DIN = 3080
EPS = 1e-6
NCORES = 8
ENGS = ("pe", "act", "dve", "pool", "sp")
MAX_SWDGE = 3

PV_PREMIX, PV_POSTMIX, PV_PREFFN, PV_POSTFFN, PV_MIXG = 0, 8, 16, 24, 32
PV_CWM, PV_CBM, PV_CWF, PV_CBF, PV_PSCALE, PV_INVW = 40, 72, 80, 212, 256, 258
NPV = 260


class Buf:
    __slots__ = ("name", "writer", "readers")

    def __init__(self, name):
        self.name = name
        self.writer = None
        self.readers = []


class Instr:
    __slots__ = ("eng", "fn", "deps", "dwaits", "signal", "sigval", "dma_key", "finish")

    def __init__(self, eng, fn, dma_key=None):
        self.eng = eng
        self.fn = fn
        self.deps = []
        self.dwaits = {}
        self.signal = False
        self.sigval = 0
        self.dma_key = dma_key
        self.finish = 0.0


class Prog:
    def __init__(self, nc, stack):
        self.nc = nc
        self.stack = stack
        self.E = {"pe": nc.tensor, "act": nc.scalar, "dve": nc.vector, "pool": nc.gpsimd, "sp": nc.sync}
        self.instrs = []
        self.dma_count = {}
        self.sems = {}
        self.last = {e: None for e in ENGS}
        self.efree = {e: 0.0 for e in ENGS}
        self.pool_dmas = []
        self.defer = None
        self.bar = {e: None for e in ENGS}

    def _sem(self, key):
        if key not in self.sems:
            self.sems[key] = self.stack.enter_context(self.nc.semaphore("s_" + str(key)))
        return self.sems[key]

    def barrier(self):
        tmax = max(self.efree.values())
        for e in ENGS:
            self.efree[e] = tmax
        lasts = [i for i in self.last.values() if i is not None]
        snap = dict(self.dma_count)
        for e in ENGS:
            self.bar[e] = (lasts, snap)

    def _producers(self, eng, reads, writes, dma_key):
        out = []
        strict = eng != "pe"
        for b in reads:
            if b.writer is not None:
                out.append(b.writer)
        for b in writes:
            w = b.writer
            if w is not None and (w.dma_key is not None or w.eng != eng or dma_key is not None or strict):
                out.append(w)
            for r in b.readers:
                if r.dma_key is not None or r.eng != eng or dma_key is not None or strict:
                    out.append(r)
        return out

    def peek_start(self, desc):
        eng, fn, reads, writes, dma_key, dur = desc[:6]
        t = self.efree[eng]
        for p in self._producers(eng, reads, writes, dma_key):
            t = max(t, p.finish + (0.2 if p.eng != eng or p.dma_key is not None else 0.05))
        return t

    def op(self, eng, fn, reads=(), writes=(), dma_key=None, dur=0.5, atomic=False):
        if self.defer is not None:
            self.defer.append((eng, fn, tuple(reads), tuple(writes), dma_key, dur, atomic))
            return None
        return self._op(eng, fn, reads, writes, dma_key, dur)

    def _op(self, eng, fn, reads=(), writes=(), dma_key=None, dur=0.5):
        start = self.peek_start((eng, fn, reads, writes, dma_key, dur))
        ins = Instr(eng, fn, dma_key)
        if dma_key is not None:
            self.efree[eng] = start + 0.6
            ins.finish = start + 3.5
        else:
            ins.finish = start + dur
            self.efree[eng] = ins.finish
        deps = {}

        def add(p):
            if p is not None:
                deps[id(p)] = p

        for b in reads:
            add(b.writer)
        strict = eng != "pe"
        for b in writes:
            w = b.writer
            if w is not None and (w.dma_key is not None or w.eng != eng or dma_key is not None or strict):
                add(w)
            for r in b.readers:
                if r.dma_key is not None or r.eng != eng or dma_key is not None or strict:
                    add(r)
        if self.bar[eng] is not None:
            lasts, snap = self.bar[eng]
            self.bar[eng] = None
            for p in lasts:
                if p.eng != eng or dma_key is not None or eng == "pool":
                    add(p)
            for k, c in snap.items():
                ins.dwaits[k] = max(ins.dwaits.get(k, 0), c)
        for p in deps.values():
            if p.dma_key is not None:
                ins.dwaits[p.dma_key] = max(ins.dwaits.get(p.dma_key, 0), self.dma_count[p.dma_key])
            else:
                p.signal = True
                ins.deps.append(p)
        if dma_key is not None:
            self.dma_count[dma_key] = self.dma_count.get(dma_key, 0) + 1
            if eng == "pool":
                if len(self.pool_dmas) >= MAX_SWDGE:
                    k_, c_ = self.pool_dmas[-MAX_SWDGE]
                    ins.dwaits[k_] = max(ins.dwaits.get(k_, 0), c_)
                self.pool_dmas.append((dma_key, self.dma_count[dma_key]))
        else:
            self.last[eng] = ins
        for b in writes:
            b.writer = ins
            b.readers = []
        for b in reads:
            if dma_key is None:
                b.readers = [r for r in b.readers if r.dma_key is not None or r.eng != eng]
            b.readers.append(ins)
        self.instrs.append(ins)
        return ins

    def record(self, fn):
        assert self.defer is None
        self.defer = []
        try:
            fn()
            return self.defer
        finally:
            self.defer = None

    def run_streams(self, streams, after=None):
        after = after or {}
        pos = [0] * len(streams)
        while True:
            best = None
            for i, st in enumerate(streams):
                if pos[i] >= len(st):
                    continue
                if any(pos[j] < len(streams[j]) for j in after.get(i, ())):
                    continue
                t = self.peek_start(st[pos[i]])
                if best is None or t < best[0]:
                    best = (t, i)
            if best is None:
                break
            i = best[1]
            while True:
                d_ = streams[i][pos[i]]
                self._op(*d_[:6])
                pos[i] += 1
                if not d_[6] or pos[i] >= len(streams[i]):
                    break

    def dma(self, queue, out, in_, reads=(), writes=(), key="dma", **kw):
        return self.op(queue, lambda e: e.dma_start(out=out, in_=in_, **kw), reads, writes, dma_key=key, dur=3.5)

    def emit(self, final_wait_keys=(), final_eng="sp"):
        sigcount = {e: 0 for e in ENGS}
        waited = {e: {} for e in ENGS}
        for ins in self.instrs:
            if ins.dma_key is None and ins.signal:
                sigcount[ins.eng] += 1
                ins.sigval = sigcount[ins.eng]
        nw = 0
        for ins in self.instrs:
            eng = self.E[ins.eng]
            need = {}
            for p in ins.deps:
                k = "eng_" + p.eng
                need[k] = max(need.get(k, 0), p.sigval)
            for dk, c in ins.dwaits.items():
                k = "dma_" + str(dk)
                need[k] = max(need.get(k, 0), 16 * c)
            for k, v in need.items():
                if waited[ins.eng].get(k, 0) >= v:
                    continue
                eng.wait_ge(self._sem(k), v)
                waited[ins.eng][k] = v
                nw += 1
            bi = ins.fn(eng)
            if ins.dma_key is not None:
                bi.then_inc(self._sem("dma_" + str(ins.dma_key)), 16)
            elif ins.signal:
                bi.then_inc(self._sem("eng_" + ins.eng), 1)
        print("[kernel] sigcount", sigcount, "dma", {k: 16 * v for k, v in self.dma_count.items()})
        eng = self.E[final_eng]
        for key in final_wait_keys:
            eng.wait_ge(self._sem("dma_" + str(key)), 16 * self.dma_count[key])
        return nw, len(self.instrs)


class Arena:
    def __init__(self, ap, nwords):
        self.ap = ap
        self.n = nwords
        self.top = 0
        self.peak = 0

    def alloc(self, dtype, shape):
        ne = 1
        for d in shape:
            ne *= d
        nbytes = ne * (2 if dtype == BF16 else 4)
        words = (nbytes + 3) // 4
        words = (words + 7) // 8 * 8
        assert self.top + words <= self.n, f"arena overflow {self.top + words} > {self.n}"
        v = self.ap[:, self.top:self.top + words]
        self.top += words
        self.peak = max(self.peak, self.top)
        if dtype == BF16:
            v = v.bitcast(BF16)
        v = v[:, 0:ne]
        if len(shape) == 2:
            v = v.rearrange("p (a b) -> p a b", a=shape[0])
        elif len(shape) == 3:
            v = v.rearrange("p (a b c) -> p a b c", a=shape[0], b=shape[1])
        return v


class StopBuild(Exception):
    pass


STOP_AT = None
HF_RANGE = (0, 1)
ML_INTERLEAVE = True
SUBBANK = 1
ML_SKIP = False
ML_LIMIT = 0


def chk(name):
    if STOP_AT == name:
        raise StopBuild(name)


class TB:
    def __init__(self, ap, name):
        self.ap = ap
        self.b = Buf(name)
        self.h = None


def build_program(layers, nlayers_total=2, debug=False):
    nc = bass.Bass("TRN2", target_bir_lowering=False)
    dr = {}

    def din(name, shape):
        dr[name] = nc.dram_tensor(name, shape, F32, kind="ExternalInput").ap()
        return dr[name]

    xT_in = din("xT", [D, S])
    for l in layers:
        din(f"wfm{l}", [14, 128, KC * 128])
        din(f"wtm{l}", [128, KC * 1288])
        din(f"wout{l}", [8, 128, KC * 128])
        din(f"wup{l}", [2 * NF, 128, KC * 128])
        din(f"wdn{l}", [8, 128, NF * 128])
        din(f"pvec{l}", [128, NPV])
        din(f"gbias{l}", [128, 32])
        din(f"poolw{l}", [128, 256])
    din("cbf", [128, 1152])
    din("cf32", [128, 288])
    yT = nc.dram_tensor("yT", [D, S], F32, kind="ExternalOutput").ap()
    if debug:
        dbg_mix = nc.dram_tensor("dbg_mix", [D, S], F32, kind="ExternalOutput").ap()
        dbg_mid = nc.dram_tensor("dbg_mid", [D, S], F32, kind="ExternalOutput").ap()
        dbg_act = nc.dram_tensor("dbg_act", [DFF, S], F32, kind="ExternalOutput").ap()
        dbg_h2 = nc.dram_tensor("dbg_h2", [D, S], F32, kind="ExternalOutput").ap()
    xa = nc.dram_tensor("xa", [D, S], F32, kind="Internal").ap()
    xb = nc.dram_tensor("xb", [D, S], F32, kind="Internal").ap()

    with ExitStack() as st:
        P = Prog(nc, st)
        NW = 48640
        arena_t = st.enter_context(nc.sbuf_tensor("arena", [128, NW], F32))
        A = Arena(arena_t, NW)
        pbt = [st.enter_context(nc.psum_tensor(f"pb{i}", [128, 512], F32)) for i in range(8)]
        pb = [TB(t, f"pb{i}") for i, t in enumerate(pbt)]

        def new(dtype, shape, name):
            return TB(A.alloc(dtype, shape), name)

        cbf = new(BF16, [1152], "cbf")
        cf32 = new(F32, [288], "cf32")
        ident = cbf.ap[:, 0:128]
        tri_neg = cbf.ap[:, 128:256]
        ones_neg = cbf.ap[:, 256:384]
        ones_bf = cbf.ap[:, 384:512]
        mask_ml = cbf.ap[:, 512:640]
        sbmask = cbf.ap[:, 640:1152]
        tri_f = cf32.ap[:, 0:128]
        ones_f = cf32.ap[:, 128:256]
        poolcorr = cf32.ap[:, 256:288].rearrange("p (b t) -> p b t", b=2)
        pvec = new(F32, [NPV], "pvec")
        gbias = new(F32, [32], "gbias")
        poolw = new(BF16, [2, 128], "poolw")
        Cst = new(F32, [4, 132], "Cst")
        Cbf = new(BF16, [4, 132], "Cbf")
        hal_m = new(F32, [8, 4], "hal_m")
        hal_f = new(F32, [2 * NF, 2], "hal_f")
        hal_f_b = [Buf(f"hal_f{i}") for i in range(4)]
        hal_p = new(F32, [2, 16], "hal_p")
        k_sb = new(BF16, [2, S], "k_sb")
        v_sb = new(BF16, [S // 128, 256], "v_sb")
        pv = pvec.ap

        P.dma("pool", cbf.ap, dr["cbf"], writes=[cbf.b], key="const")
        P.dma("sp", cf32.ap, dr["cf32"], writes=[cf32.b], key="constf")

        def _fs(ap):
            n = 1
            for d_ in list(ap.shape)[1:]:
                n *= int(d_)
            return n

        def ACT(out, in_, func, reads, writes, scale=1.0, bias=0.0, accum=None):
            kw = {}
            if accum is not None:
                kw["accum_out"] = accum
            du = 0.2 + 0.00095 * _fs(out)
            if func == AF.Copy:
                return P.op("act", lambda e: e.activation(out=out, in_=in_, func=func, scale=scale, **kw), reads, writes, dur=du)
            return P.op("act", lambda e: e.activation(out=out, in_=in_, func=func, scale=scale, bias=bias, **kw), reads, writes, dur=du)

        def TT(eng, out, a, b, op, reads, writes):
            du = (0.12 + 0.0011 * _fs(out)) if eng == "dve" else (0.3 + 0.002 * _fs(out))
            return P.op(eng, lambda e: e.tensor_tensor(out=out, in0=a, in1=b, op=op), reads, writes, dur=du)

        def TS(eng, out, a, s1, s2, op0, op1, reads, writes):
            du = 0.12 + 0.0011 * _fs(out)
            if s2 is None:
                return P.op(eng, lambda e: e.tensor_scalar(out=out, in0=a, scalar1=s1, scalar2=None, op0=op0), reads, writes, dur=du)
            return P.op(eng, lambda e: e.tensor_scalar(out=out, in0=a, scalar1=s1, scalar2=s2, op0=op0, op1=op1), reads, writes, dur=du)

        def STT(out, a, s, b, op0, op1, reads, writes):
            return P.op("dve", lambda e: e.scalar_tensor_tensor(out=out, in0=a, scalar=s, in1=b, op0=op0, op1=op1), reads, writes,
                        dur=0.12 + 0.0012 * _fs(out))

        def CP(eng, out, in_, reads, writes):
            du = (0.1 + 0.0009 * _fs(out)) if eng == "dve" else (0.3 + 0.002 * _fs(out))
            return P.op(eng, lambda e: e.tensor_copy(out=out, in_=in_), reads, writes, dur=du)

        def MS(eng, out, val, writes):
            return P.op(eng, lambda e: e.memset(out, val), [], writes, dur=0.3)

        def MM(out, lhsT, rhs, start, stop, reads, writes, sgc=False):
            return P.op("pe", lambda e: e.matmul(out, lhsT=lhsT, rhs=rhs, start=start, stop=stop, skip_group_check=sgc), reads, writes,
                        dur=max(_fs(out), 64) / 2400.0 * 1.3, atomic=(not stop) and (not sgc))

        def TR(out, in_, reads, writes):
            return P.op("pe", lambda e: e.transpose(out, in_, ident), list(reads) + [cbf.b], writes, dur=0.15)

        def xview(ap, q):
            return ap.rearrange("(k p) t -> p k t", p=128)[:, :, q * G:(q + 1) * G]

        class WStream:
            def __init__(self, slots, keybase, hold=1):
                self.hold = hold
                self.slots = slots
                self.keybase = keybase
                self.tasks = []
                self.issued = 0
                self.used = 0

            def add(self, dram_ap, ncols):
                self.tasks.append((dram_ap, ncols))

            def _issue(self):
                i = self.issued
                src, ncols = self.tasks[i]
                s = self.slots[i % len(self.slots)]
                dst = s.ap.rearrange("p a b -> p (a b)")[:, 0:ncols] if len(s.ap.shape) == 3 else s.ap[:, 0:ncols]
                P.dma("pool", dst, src, writes=[s.b], key=f"{self.keybase}{i % len(self.slots)}", max_dma_last_dim=4096)
                self.issued += 1

            def get(self):
                while self.issued < len(self.tasks) and self.issued < self.used + len(self.slots) - (self.hold - 1):
                    self._issue()
                s = self.slots[self.used % len(self.slots)]
                self.used += 1
                return s

        def rstd_from_ssq(ps, width, dst, reads, n=G):
            ACT(dst.ap[:, 0:n], ps, AF.Ln, reads, [dst.b], scale=1.0 / width, bias=EPS)
            ACT(dst.ap[:, 0:n], dst.ap[:, 0:n], AF.Exp, [dst.b], [dst.b], scale=-0.5)

        def norm_fm_parts(src, nblk, width, gcol, dst_ap, dst_b, sq, rstd, bank):
            def p1():
                ACT(sq.ap[:, 0:nblk, :], src.ap[:, 0:nblk, :], AF.Square, [src.b], [sq.b])

            def p2():
                for j in range(nblk):
                    MM(bank.ap[:, 0:G], ones_bf, sq.ap[:, j, :], j == 0, j == nblk - 1, [sq.b, cbf.b], [bank.b])

            def p3():
                rstd_from_ssq(bank.ap[:, 0:G], width, rstd, [bank.b])

            def p4():
                for j in range(nblk):
                    STT(dst_ap[:, j, :], src.ap[:, j, :], pv[:, gcol + j:gcol + j + 1], rstd.ap[:, 0:G], ALU.mult, ALU.mult,
                        [src.b, rstd.b, pvec.b], [dst_b])
            return [p1, p2, p3, p4]

        def norm_fm(*a):
            for f_ in norm_fm_parts(*a):
                f_()

        def resid_parts(xg, ybuf, gcol, sq, rstd, tmp, bank):
            def p1():
                ACT(sq.ap, ybuf.ap, AF.Square, [ybuf.b], [sq.b])

            def p2():
                for j in range(KC):
                    MM(bank.ap[:, 0:G], ones_bf, sq.ap[:, j, :], j == 0, j == KC - 1, [sq.b, cbf.b], [bank.b])

            def p3():
                rstd_from_ssq(bank.ap[:, 0:G], D, rstd, [bank.b])

            def p4(j0, j1):
                for j in range(j0, j1):
                    t = tmp[j % 2]
                    STT(t.ap, ybuf.ap[:, j, :], pv[:, gcol + j:gcol + j + 1], rstd.ap[:, 0:G], ALU.mult, ALU.mult,
                        [ybuf.b, rstd.b, pvec.b], [t.b])
                    TT("dve", xg.ap[:, j, :], xg.ap[:, j, :], t.ap, ALU.add, [xg.b, t.b], [xg.b])
            return [p1, p2, p3, lambda: p4(0, 4), lambda: p4(4, 8)]

        def resid_update(*a):
            for f_ in resid_parts(*a):
                f_()

        persist_top = A.top
        try:
          for li, l in enumerate(layers):
              gl = l
              if li == 0:
                  Xin = xT_in
              else:
                  Xin = xb
              Xout = yT if li == len(layers) - 1 else xb
              Xmid = xa
              bXin = xin_bufs if li > 0 else [Buf(f"xin{q}") for q in range(NGR)]
              bXmid = [Buf(f"xmid{l}_{q}") for q in range(NGR)]
              bXout = [Buf(f"xout{l}_{q}") for q in range(NGR)]

              P.barrier()
              P.dma("sp", pvec.ap, dr[f"pvec{l}"], writes=[pvec.b], key="constf")
              P.dma("sp", gbias.ap, dr[f"gbias{l}"], writes=[gbias.b], key="constf")
              P.dma("pool", poolw.ap.rearrange("p a b -> p (a b)"), dr[f"poolw{l}"], writes=[poolw.b], key="const")
              MS("pool", hal_m.ap, 0.0, [hal_m.b])
              MS("pool", hal_f.ap, 0.0, hal_f_b)
              MS("pool", Cst.ap, 0.0, [Cst.b])
              MS("pool", Cbf.ap, 0.0, [Cbf.b])

              A.top = persist_top
              qk_m = new(BF16, [8, G], "qk_m")
              qzs = [new(BF16, [4, G], f"qz{i}") for i in range(2)]
              up = new(F32, [2, 16 + G], "up")
              v_aug = new(BF16, [4, 4, 130], "v_aug")
              sig_o = new(BF16, [4, 512], "sig_o")
              gpre = new(F32, [4, 8], "gpre")
              mixTs = [new(BF16, [8, G], f"mixT{i}") for i in range(2)]
              xg = new(F32, [8, G], "xg")
              sq = new(BF16, [8, G], "sq")
              rstd = new(F32, [G], "rstd")
              hT = new(BF16, [8, G], "hT")
              wfm = [new(BF16, [8, 128], f"wfm{i}") for i in range(3)]
              wtm = [new(BF16, [8, 264], f"wtm{i}") for i in range(2)]
              raw = [new(F32, [G + 4], f"raw{i}") for i in range(2)]
              acc = [new(F32, [G], f"acc{i}") for i in range(2)]
              for t_ in raw:
                  t_.h = Buf("rawh")
              xg3 = xg
              ybuf = new(F32, [8, G], "ybuf")
              tmp = [new(F32, [G], f"tmp{i}") for i in range(2)]
              gsp = new(F32, [4, 4], "gsp")
              gu = new(F32, [4, 4], "gu")
              gw = new(F32, [4, 4], "gw")
              ge = new(F32, [4, 4], "ge")
              mST = [new(BF16, [128], f"mST{i}") for i in range(2)]
              ktok = [new(BF16, [128], f"ktok{i}") for i in range(2)]
              vp = [new(BF16, [130], f"vp{i}") for i in range(2)]
              dn = [new(F32, [4], f"dn{i}") for i in range(2)]
              Ct = [new(F32, [132], f"Ct{i}") for i in range(2)]
              ym = [new(F32, [512], f"ym{i}") for i in range(2)]
              ymn = [new(BF16, [512], f"ymn{i}") for i in range(2)]
              junk = new(BF16, [512], "junk")
              ss = [new(F32, [2], f"ss{i}") for i in range(2)]
              ebuf = [new(F32, [G], f"e{i}") for i in range(2)]
              spb = [new(BF16, [G], f"sp{i}") for i in range(3)]
              aT = [new(BF16, [G], f"aT{i}") for i in range(3)]
              accr = [[new(BF16, [G], f"accr{h_}_{i}") for i in range(3)] for h_ in range(2)]
              ysb = new(F32, [2, G], "ysb")
              sq2 = new(BF16, [2, G], "sq2")
              rstd2 = new(F32, [G], "rstd2")
              s_a = new(F32, [16 + G], "s_a")
              s_b = new(F32, [16 + G], "s_b")
              ypT = new(BF16, [2, G], "ypT")
              ypo = new(F32, [2, G], "ypo")
              ksb_b = [Buf(f"ksb{i}") for i in range(NGR)]
              vsb_b = [Buf(f"vsb{i}") for i in range(NGR)]

              def sub(bank, ap, name):
                  t_ = TB(ap, name)
                  t_.b = bank.b
                  return t_
              pG = sub(pb[7], pb[7].ap[:, 448:480], "pG")
              pST = sub(pb[6], pb[6].ap[:, 0:128], "pST")
              pKT = sub(pb[6], pb[6].ap.bitcast(BF16)[:, 256:384], "pKT")
              pTt = sub(pb[6], pb[6].ap.bitcast(BF16)[:, 512:1024], "pTt")
              pH = sub(pb[7], pb[7].ap[:, 0:129], "pH")
              pC = sub(pb[7], pb[7].ap[:, 256:385], "pC")
              wtm_d = dr[f"wtm{l}"].rearrange("p (k c) -> p k c", k=KC)
              tmcols = [(0, 256, "v", 0), (256, 256, "v", 1), (512, 256, "o", 0), (768, 256, "o", 1), (1024, 264, "s", 0)]
              sbank = [pb[4], pb[5]]

              def s1_steps(g):
                  T0 = g * G
                  qz = qzs[g % 2]
                  steps = []
                  ws = WStream(wfm, "wfm")
                  for bi in range(14):
                      ws.add(dr[f"wfm{l}"][bi], KC * 128)
                  state = {"wt_issued": 0, "nb": 0}

                  def st_norm():
                      P.dma("sp", xg.ap, xview(Xin, g), reads=[bXin[g]], writes=[xg.b], key="ldx")
                      MS("pool", qz.ap, 0.0, [qz.b])
                  steps.append(st_norm)
                  steps.extend(norm_fm_parts(xg, KC, D, PV_PREMIX, hT.ap, hT.b, sq, rstd, pb[4]))

                  def st_fm_taps(bi):
                      rb = raw[bi % 2]
                      ab = acc[bi % 2]
                      cw = PV_CWM + bi * 4
                      TS("dve", ab.ap, rb.ap[:, 3:3 + G], pv[:, cw + 3:cw + 4], pv[:, PV_CBM + bi:PV_CBM + bi + 1],
                         ALU.mult, ALU.add, [rb.b, pvec.b], [ab.b])
                      for j in (2, 1, 0):
                          STT(ab.ap, rb.ap[:, j:j + G], pv[:, cw + j:cw + j + 1], ab.ap, ALU.mult, ALU.add,
                              [rb.b, rb.h, ab.b, pvec.b], [ab.b])

                  def st_fm_silu(bi):
                      ACT(qk_m.ap[:, bi, :], acc[bi % 2].ap, AF.Silu, [acc[bi % 2].b], [qk_m.b])

                  def st_fm(bi):
                      wsl = ws.get()
                      bank = sbank[bi % 2]
                      for k in range(KC):
                          MM(bank.ap[:, 0:G], wsl.ap[:, k, :], hT.ap[:, k, :], k == 0, k == KC - 1, [wsl.b, hT.b], [bank.b])
                      if bi < 8:
                          rb = raw[bi % 2]
                          ACT(rb.ap[:, 0:3], hal_m.ap[:, bi, 0:3], AF.Copy, [hal_m.b], [rb.h])
                          ACT(rb.ap[:, 3:3 + G], bank.ap[:, 0:G], AF.Copy, [bank.b], [rb.b])
                          ACT(hal_m.ap[:, bi, 0:3], rb.ap[:, G:G + 3], AF.Copy, [rb.b], [hal_m.b])
                      elif bi < 10:
                          b = bi - 8
                          ACT(qz.ap[0:64, 2 * b, :], bank.ap[0:64, 0:G], AF.Copy, [bank.b], [qz.b], scale=0.125)
                          ACT(qz.ap[64:128, 2 * b + 1, :], bank.ap[64:128, 0:G], AF.Copy, [bank.b], [qz.b], scale=0.125)
                      elif bi < 12:
                          b = bi - 10
                          ACT(k_sb.ap[:, b, T0:T0 + G], bank.ap[:, 0:G], AF.Copy, [bank.b], [ksb_b[g]])
                      else:
                          b = bi - 12
                          if b == 0:
                              if g == 0:
                                  MS("pool", up.ap[:, :, 0:16], 0.0, [up.b])
                              else:
                                  CP("pool", up.ap[:, :, 0:16], hal_p.ap, [hal_p.b], [up.b])
                          ACT(up.ap[:, b, 16:16 + G], bank.ap[:, 0:G], AF.Copy, [bank.b], [up.b])
                  for bi in range(14):
                      steps.append((lambda bi_: (lambda: st_fm(bi_)))(bi))
                      if bi < 8:
                          steps.append((lambda bi_: (lambda: st_fm_taps(bi_)))(bi))
                          steps.append((lambda bi_: (lambda: st_fm_silu(bi_)))(bi))

                  def wt_get(ci):
                      while state["wt_issued"] < len(tmcols) and state["wt_issued"] <= ci + 1:
                          i_ = state["wt_issued"]
                          c0_, nc_, _, _ = tmcols[i_]
                          s_ = wtm[i_ % 2]
                          P.dma("pool", s_.ap[:, :, 0:nc_], wtm_d[:, :, c0_:c0_ + nc_], writes=[s_.b], key=f"wtm{i_ % 2}",
                                max_dma_last_dim=4096)
                          state["wt_issued"] += 1
                      return wtm[ci % 2]

                  def st_tm(ci, i):
                      c0, ncol, kind, half = tmcols[ci]
                      if i == 0:
                          if ci == 0:
                              MS("pool", v_aug.ap, 1.0, [v_aug.b])
                      wsl = wt_get(ci) if i == 0 else wtm[ci % 2]
                      bank = sbank[state["nb"] % 2]
                      state["nb"] += 1
                      for k in range(KC):
                          MM(bank.ap[:, 0:ncol], hT.ap[:, k, i * 128:(i + 1) * 128], wsl.ap[:, k, 0:ncol], k == 0, k == KC - 1,
                             [wsl.b, hT.b], [bank.b])
                      if kind == "v":
                          ACT(v_aug.ap[:, i, 2 * half:2 * half + 2, 0:128], bank.ap[:, 0:256].rearrange("p (h d) -> p h d", h=2), AF.Copy,
                              [bank.b], [v_aug.b])
                      elif kind == "o":
                          ACT(sig_o.ap[:, i, 256 * half:256 * half + 256], bank.ap[:, 0:256], AF.Sigmoid, [bank.b], [sig_o.b])
                      else:
                          ACT(v_sb.ap[:, T0 // 128 + i, :], bank.ap[:, 0:256], AF.Copy, [bank.b], [vsb_b[g]])
                          TT("dve", gpre.ap[:, i, :], bank.ap[:, 256:264], gbias.ap[:, 0:8], ALU.add, [bank.b, gbias.b], [gpre.b])
                  for ci in range(len(tmcols)):
                      for i in range(4):
                          steps.append((lambda ci_, i_: (lambda: st_tm(ci_, i_)))(ci, i))
                  return steps

              def s3_steps(g):
                  mixT = mixTs[g % 2]
                  steps = []
                  ws = WStream(wfm, "wfm")
                  for m in range(8):
                      ws.add(dr[f"wout{l}"][m], KC * 128)

                  def st_ld():
                      P.dma("sp", xg3.ap, xview(Xin, g), reads=[bXin[g]], writes=[xg3.b], key="ldx")
                  steps.append(st_ld)

                  def st_m(m):
                      wsl = ws.get()
                      bank = sbank[m % 2]
                      for k in range(KC):
                          MM(bank.ap[:, 0:G], wsl.ap[:, k, :], mixT.ap[:, k, :], k == 0, k == KC - 1, [wsl.b, mixT.b], [bank.b])
                      ACT(ybuf.ap[:, m, :], bank.ap[:, 0:G], AF.Copy, [bank.b], [ybuf.b])
                  for m in range(8):
                      steps.append((lambda m_: (lambda: st_m(m_)))(m))

                  steps.extend(resid_parts(xg3, ybuf, PV_POSTMIX, sq, rstd, tmp, pb[4]))

                  def st_res():
                      P.dma("sp", xview(Xmid, g), xg3.ap, reads=[xg3.b], writes=[bXmid[g]], key="st")
                      if debug and li == 0:
                          P.dma("sp", xview(dbg_mid, g), xg3.ap, reads=[xg3.b], writes=[Buf("dbgx")], key="st")
                  steps.append(st_res)
                  return steps

              def gates(g):
                  gi = gpre.ap[:, :, 0:4]
                  gf = gpre.ap[:, :, 4:8]
                  ACT(gsp.ap, gf, AF.Exp, [gpre.b], [gsp.b], scale=-1.0)
                  ACT(gsp.ap, gsp.ap, AF.Ln, [gsp.b], [gsp.b], bias=1.0)
                  gsp2 = gsp.ap.rearrange("p a b -> p (a b)")
                  MM(pG.ap[:, 0:16], tri_f, gsp2, True, True, [gsp.b, cf32.b], [pG.b])
                  MM(pG.ap[:, 16:32], ones_f, gsp2, True, True, [gsp.b, cf32.b], [pG.b])
                  cum = pG.ap[:, 0:16].rearrange("p (a b) -> p a b", a=4)
                  tot = pG.ap[:, 16:32].rearrange("p (a b) -> p a b", a=4)
                  TT("dve", gu.ap, gi, cum, ALU.add, [gpre.b, pG.b], [gu.b])
                  ACT(gu.ap, gu.ap, AF.Exp, [gu.b], [gu.b], bias=math.log(128.0 ** -0.5))
                  ACT(gw.ap, cum, AF.Exp, [pG.b], [gw.b], scale=-1.0)
                  ACT(ge.ap, tot, AF.Exp, [pG.b], [ge.b], scale=-1.0)

              def ml_steps_for(g):
                  mixT = mixTs[g % 2]

                  def ml_u1(i, h, sl):
                      ts_ = slice(i * 128, (i + 1) * 128)
                      qT = qk_m.ap[:, h, ts_]
                      kT = qk_m.ap[:, 4 + h, ts_]
                      col = slice(h, h + 1)
                      MM(pST.ap, kT, qT, True, True, [qk_m.b], [pST.b])
                      TS("dve", vp[sl].ap[:, 0:129], v_aug.ap[:, i, h, 0:129], gu.ap[:, i, col], None, ALU.mult, None,
                         [v_aug.b, gu.b], [vp[sl].b])

                  def ml_u2(i, h, sl):
                      ts_ = slice(i * 128, (i + 1) * 128)
                      qT = qk_m.ap[:, h, ts_]
                      kT = qk_m.ap[:, 4 + h, ts_]
                      TT("dve", mST[sl].ap, pST.ap, mask_ml, ALU.mult, [pST.b, cbf.b], [mST[sl].b])
                      TR(pKT.ap, kT, [qk_m.b], [pKT.b])
                      MM(pH.ap, mST[sl].ap, vp[sl].ap[:, 0:129], True, False, [mST[sl].b, vp[sl].b], [pH.b])
                      MM(pH.ap, qT, Cbf.ap[:, h, 0:129], False, True, [qk_m.b, Cbf.b], [pH.b])
                      ACT(ktok[sl].ap, pKT.ap, AF.Copy, [pKT.b], [ktok[sl].b])

                  def ml_u3(i, h, sl):
                      ymt = ym[i % 2]
                      col = slice(h, h + 1)
                      d = dn[sl]
                      ACT(d.ap[:, 0:1], pH.ap[:, 128:129], AF.Abs, [pH.b, gw.b], [d.b], scale=gw.ap[:, i, col])
                      TS("dve", d.ap[:, 1:2], d.ap[:, 0:1], 1.0, None, ALU.max, None, [d.b], [d.b])
                      P.op("dve", (lambda dd: lambda e: e.reciprocal(out=dd.ap[:, 2:3], in_=dd.ap[:, 1:2]))(d), [d.b], [d.b])
                      TT("dve", d.ap[:, 3:4], d.ap[:, 2:3], gw.ap[:, i, col], ALU.mult, [d.b, gw.b], [d.b])
                      STT(ymt.ap[:, h * 128:(h + 1) * 128], pH.ap[:, 0:128], d.ap[:, 3:4], sig_o.ap[:, i, h * 128:(h + 1) * 128],
                          ALU.mult, ALU.mult, [pH.b, d.b, sig_o.b], [ymt.b])
                      MM(pC.ap, ktok[sl].ap, vp[sl].ap[:, 0:129], True, True, [ktok[sl].b, vp[sl].b], [pC.b])

                  def ml_u4(i, h, sl):
                      col = slice(h, h + 1)
                      TT("dve", Ct[sl].ap[:, 0:129], Cst.ap[:, h, 0:129], pC.ap, ALU.add, [Cst.b, pC.b], [Ct[sl].b])
                      ACT(Cst.ap[:, h, 0:129], Ct[sl].ap[:, 0:129], AF.Copy, [Ct[sl].b, ge.b], [Cst.b], scale=ge.ap[:, i, col])
                      ACT(Cbf.ap[:, h, 0:129], Ct[sl].ap[:, 0:129], AF.Copy, [Ct[sl].b, ge.b], [Cbf.b], scale=ge.ap[:, i, col])

                  def ml_fin(i):
                      ts_ = slice(i * 128, (i + 1) * 128)
                      ymt = ym[i % 2]
                      s1 = ss[i % 2]
                      ACT(junk.ap, ymt.ap, AF.Square, [ymt.b], [junk.b, s1.b], accum=s1.ap[:, 0:1])
                      ACT(s1.ap[:, 1:2], s1.ap[:, 0:1], AF.Ln, [s1.b], [s1.b], scale=1.0 / 512, bias=EPS)
                      ACT(s1.ap[:, 1:2], s1.ap[:, 1:2], AF.Exp, [s1.b], [s1.b], scale=-0.5)
                      ACT(ymn[i % 2].ap, ymt.ap, AF.Copy, [ymt.b, s1.b], [ymn[i % 2].b], scale=s1.ap[:, 1:2])
                      for h in range(4):
                          TR(pTt.ap[:, h * 128:(h + 1) * 128], ymn[i % 2].ap[:, h * 128:(h + 1) * 128], [ymn[i % 2].b], [pTt.b])
                      for h in range(4):
                          ACT(mixT.ap[:, h, ts_], pTt.ap[:, h * 128:(h + 1) * 128], AF.Copy, [pTt.b, pvec.b], [mixT.b],
                              scale=pv[:, PV_MIXG + h:PV_MIXG + h + 1])

                  steps = []
                  cnt = 0
                  for i in range(4):
                      for h in range(4):
                          for fn__ in (ml_u1, ml_u2, ml_u3, ml_u4):
                              steps.append((lambda f_, a_: (lambda: f_(*a_)))(fn__, (i, h, cnt % 2)))
                          cnt += 1
                      steps.append((lambda i_: (lambda: ml_fin(i_)))(i))
                  return steps

              def pool_step(g):
                  mixT = mixTs[g % 2]
                  for b in range(2):
                      ub = up.ap[:, b, :]
                      W_ = 16 + G
                      TT("pool", s_a.ap[:, 1:W_], ub[:, 1:W_], ub[:, 0:W_ - 1], ALU.add, [up.b], [s_a.b])
                      if b == 0:
                          TT("pool", s_b.ap[64:128, 3:W_], s_a.ap[64:128, 3:W_], s_a.ap[64:128, 1:W_ - 2], ALU.add, [s_a.b], [s_b.b])
                      else:
                          TT("pool", s_b.ap[:, 3:W_], s_a.ap[:, 3:W_], s_a.ap[:, 1:W_ - 2], ALU.add, [s_a.b], [s_b.b])
                          TT("pool", s_a.ap[:, 7:W_], s_b.ap[:, 7:W_], s_b.ap[:, 3:W_ - 4], ALU.add, [s_b.b], [s_a.b])
                          TT("pool", s_b.ap[64:128, 15:W_], s_a.ap[64:128, 15:W_], s_a.ap[64:128, 7:W_ - 8], ALU.add, [s_a.b], [s_b.b])
                      for (src, ps_) in ((s_a, slice(0, 64)), (s_b, slice(64, 128))):
                          if g == 0:
                              TT("dve", src.ap[ps_, 16:32], src.ap[ps_, 16:32], poolcorr[ps_, b, :], ALU.mult, [src.b, cf32.b], [src.b])
                          STT(ypT.ap[ps_, b, :], src.ap[ps_, 16:W_], pv[ps_, PV_INVW + b:PV_INVW + b + 1], ub[ps_, 16:W_],
                              ALU.mult, ALU.subtract, [src.b, up.b, pvec.b], [ypT.b])
                      bank = pb[6 + b]
                      MM(bank.ap[:, 0:G], poolw.ap[:, b, :], ypT.ap[:, b, :], True, True, [poolw.b, ypT.b], [bank.b])
                      ACT(ypo.ap[:, b, :], bank.ap[:, 0:G], AF.Copy, [bank.b, pvec.b], [ypo.b], scale=pv[:, PV_PSCALE + b:PV_PSCALE + b + 1])
                  CP("pool", hal_p.ap, up.ap[:, :, G:G + 16], [up.b], [hal_p.b])
                  norm_fm(ypo, 2, 256, PV_MIXG + 6, mixT.ap[:, 6:8, :], mixT.b, sq2, rstd2, pb[7])

              def sb_loop(g):
                  mixT = mixTs[g % 2]
                  qz = qzs[g % 2]
                  nj = 4 * g + 4
                  pairs = []
                  for b in range(2):
                      for hh in range(2):
                          for idx, j in enumerate(reversed(range(nj))):
                              pairs.append((b, hh, j, idx))

                  def sb_geom(p):
                      b, hh, j, idx = pairs[p]
                      r = j - 4 * g
                      tl = 128 * r if r >= 0 else 0
                      return b, hh, j, idx, r, tl, G - tl

                  def stageA(p):
                      b, hh, j, idx, r, tl, N = sb_geom(p)
                      hd = 2 * b + hh
                      zs, s3 = p % 2, p % 3
                      ring = accr[hh]
                      if idx == 0:
                          for t_ in ring:
                              MS("pool", t_.ap, 0.0, [t_.b])
                      pZ = pb[0]
                      kTj = k_sb.ap[:, b, j * 128:(j + 1) * 128]
                      qh = qz.ap[:, hd, tl:G]
                      MM(pZ.ap[:, 0:N], kTj, qh, True, True, [ksb_b[j // 4], qz.b], [pZ.b])
                      ACT(ebuf[zs].ap[:, 0:N], pZ.ap[:, 0:N], AF.Exp, [pZ.b], [ebuf[zs].b])
                      ACT(spb[s3].ap[:, 0:N], ebuf[zs].ap[:, 0:N], AF.Ln, [ebuf[zs].b], [spb[s3].b], bias=1.0)
                      if r >= 0:
                          TT("dve", spb[s3].ap[:, 0:N], spb[s3].ap[:, 0:N], sbmask[:, 0:N], ALU.mult, [spb[s3].b, cbf.b], [spb[s3].b])
                      if j > 0:
                          src, dst = ring[idx % 3], ring[(idx + 1) % 3]
                          TT("dve", dst.ap[:, tl:G], src.ap[:, tl:G], spb[s3].ap[:, 0:N], ALU.add, [src.b, spb[s3].b], [dst.b])

                  def stageB(p):
                      b, hh, j, idx, r, tl, N = sb_geom(p)
                      hd = 2 * b + hh
                      zs, s3 = p % 2, p % 3
                      ring = accr[hh]
                      first = idx == 0
                      pZR = pb[1 + zs]
                      kTj = k_sb.ap[:, b, j * 128:(j + 1) * 128]
                      qh = qz.ap[:, hd, tl:G]
                      MM(pZR.ap[:, 0:N], kTj, qh, True, False, [ksb_b[j // 4], qz.b], [pZR.b])
                      MM(pZR.ap[:, 0:N], tri_neg, spb[s3].ap[:, 0:N], False, first, [spb[s3].b, cbf.b], [pZR.b])
                      if not first:
                          cur = ring[idx % 3]
                          MM(pZR.ap[:, 0:N], ones_neg, cur.ap[:, tl:G], False, True, [cur.b, cbf.b], [pZR.b])
                      ACT(aT[s3].ap[:, 0:N], pZR.ap[:, 0:N], AF.Exp, [pZR.b], [aT[s3].b])
                      if r >= 0:
                          TT("dve", aT[s3].ap[:, 0:N], aT[s3].ap[:, 0:N], sbmask[:, 0:N], ALU.mult, [aT[s3].b, cbf.b], [aT[s3].b])

                  def stageC(p):
                      b, hh, j, idx, r, tl, N = sb_geom(p)
                      s3 = p % 3
                      first = idx == 0
                      pO = pb[3]
                      MM(pO.ap[:, tl:G], v_sb.ap[:, j, b * 128:(b + 1) * 128], aT[s3].ap[:, 0:N], first, j == 0,
                         [vsb_b[j // 4], aT[s3].b], [pO.b], sgc=True)
                      if j == 0:
                          ps_ = slice(64 * hh, 64 * hh + 64)
                          ACT(ysb.ap[ps_, b, :], pO.ap[ps_, 0:G], AF.Copy, [pO.b], [ysb.b])

                  stageA(0)
                  for p in range(len(pairs)):
                      if p + 1 < len(pairs):
                          stageA(p + 1)
                      stageB(p)
                      if p >= 1:
                          stageC(p - 1)
                  stageC(len(pairs) - 1)

              def run_all(fns):
                  for f_ in fns:
                      f_()
              for st_ in s1_steps(0):
                  st_()
              for g in range(NGR):
                  gates(g)
                  S_sb = P.record(lambda: sb_loop(g))
                  S_a1 = P.record(lambda: run_all(s3_steps(g - 1))) if g > 0 else []
                  S_b = P.record(lambda: (pool_step(g), run_all(ml_steps_for(g))))
                  S_a2 = P.record(lambda: run_all(s1_steps(g + 1))) if g + 1 < NGR else []
                  P.run_streams([S_sb, S_a1, S_b, S_a2], after={3: (1, 2)})
                  mixT = mixTs[g % 2]
                  norm_fm(ysb, 2, 256, PV_MIXG + 4, mixT.ap[:, 4:6, :], mixT.b, sq2, rstd2, pb[0])
                  chk(f"S3_{g}")
              for st_ in s3_steps(NGR - 1):
                  st_()
              P.barrier()
              chk("mixer")
              for hf in HF_RANGE:
                  A.top = persist_top
                  hT2 = new(BF16, [8, 2 * G], "hT2")
                  actT = new(BF16, [NF, 2 * G], "actT")
                  xgs = [new(F32, [8, G], f"xgf{i}") for i in range(2)]
                  sqs = [new(BF16, [8, G], f"sqf{i}") for i in range(2)]
                  rstds = [new(F32, [G], f"rstdf{i}") for i in range(2)]
                  markf = A.top
                  wfm = [new(BF16, [8, 128], f"wu{i}") for i in range(4)]
                  NSL = 4
                  rawg = [new(F32, [G + 4], f"rawg{i}") for i in range(NSL)]
                  rawv = [new(F32, [G + 4], f"rawv{i}") for i in range(NSL)]
                  accg = [new(F32, [G], f"accg{i}") for i in range(NSL)]
                  accv = [new(F32, [G], f"accv{i}") for i in range(NSL)]
                  gel = [new(F32, [G], f"gel{i}") for i in range(NSL)]
                  for t_ in rawg + rawv:
                      t_.h = Buf("rwh")
                  for qq in range(2):
                      Q = 2 * hf + qq
                      P.dma("sp", xgs[qq].ap, xview(Xmid, Q), reads=[bXmid[Q]], writes=[xgs[qq].b], key=f"ldxf{qq}")
                  for qq in range(2):
                      norm_fm(xgs[qq], KC, D, PV_PREFFN, hT2.ap[:, :, qq * G:(qq + 1) * G], hT2.b, sqs[qq], rstds[qq], pb[6 + qq])
                  chk("ffn_norm")
                  ws = WStream(wfm, "wfm", hold=2)
                  for f in range(NF):
                      ws.add(dr[f"wup{l}"][f], KC * 128)
                      ws.add(dr[f"wup{l}"][NF + f], KC * 128)
                  items = []

                  def ffn_front(f, qq, wg, wv, n):
                      Q = 2 * hf + qq
                      sl = n % NSL
                      bsl = n % 4
                      pg, pvb = pb[2 * bsl], pb[2 * bsl + 1]
                      cs = slice(qq * G, (qq + 1) * G)
                      for k in range(KC):
                          MM(pg.ap[:, 0:G], wg.ap[:, k, :], hT2.ap[:, k, cs], k == 0, k == KC - 1, [wg.b, hT2.b], [pg.b])
                      for k in range(KC):
                          MM(pvb.ap[:, 0:G], wv.ap[:, k, :], hT2.ap[:, k, cs], k == 0, k == KC - 1, [wv.b, hT2.b], [pvb.b])
                      for (fb, bank, rw, ac) in ((f, pg, rawg[sl], accg[sl]), (NF + f, pvb, rawv[sl], accv[sl])):
                          cw = PV_CWF + fb * 3
                          hb = hal_f_b[fb % 4]
                          ACT(rw.ap[:, 0:2], hal_f.ap[:, fb, :], AF.Copy, [hb], [rw.h])
                          ACT(rw.ap[:, 2:2 + G], bank.ap[:, 0:G], AF.Copy, [bank.b], [rw.b])
                          ACT(hal_f.ap[:, fb, :], rw.ap[:, G:G + 2], AF.Copy, [rw.b], [hb])
                          ACT(ac.ap, bank.ap[:, 0:G], AF.Identity, [bank.b, pvec.b], [ac.b], scale=pv[:, cw + 2:cw + 3],
                              bias=pv[:, PV_CBF + fb:PV_CBF + fb + 1])
                          for j in (1, 0):
                              STT(ac.ap, rw.ap[:, j:j + G], pv[:, cw + j:cw + j + 1], ac.ap, ALU.mult, ALU.add,
                                  [rw.b, rw.h, ac.b, pvec.b], [ac.b])

                  def ffn_back(f, qq, n):
                      sl = n % NSL
                      cs = slice(qq * G, (qq + 1) * G)
                      ACT(gel[sl].ap, accg[sl].ap, AF.Gelu_apprx_tanh, [accg[sl].b], [gel[sl].b])
                      TT("dve", actT.ap[:, f, cs], gel[sl].ap, accv[sl].ap, ALU.mult, [gel[sl].b, accv[sl].b], [actT.b])

                  cnt = 0
                  prev = None
                  for f in range(NF):
                      wg = ws.get()
                      wv = ws.get()
                      for qq in range(2):
                          ffn_front(f, qq, wg, wv, cnt)
                          if prev is not None:
                              ffn_back(*prev)
                          prev = (f, qq, cnt)
                          cnt += 1
                      chk(f"ffn_f{f}")
                  ffn_back(*prev)
                  if debug and li == 0:
                      P.dma("pool", dbg_act.rearrange("(k p) t -> p k t", p=128)[:, :, hf * 2 * G:(hf + 1) * 2 * G], actT.ap,
                            reads=[actT.b], writes=[Buf("dbga")], key="st")
                      P.dma("pool", dbg_h2.rearrange("(k p) t -> p k t", p=128)[:, :, hf * 2 * G:(hf + 1) * 2 * G], hT2.ap,
                            reads=[hT2.b], writes=[Buf("dbgh")], key="st")
                  chk("ffn_up")
                  P.barrier()
                  A.top = markf
                  ybuf2 = [new(F32, [8, G], f"ybuff{i}") for i in range(2)]
                  tmps = [[new(F32, [G], f"tmpf{q_}_{i}") for i in range(2)] for q_ in range(2)]
                  wdn = [new(BF16, [NF, 128], f"wd{i}") for i in range(2)]
                  wd = WStream(wdn, "wdn")
                  for m in range(8):
                      wd.add(dr[f"wdn{l}"][m], NF * 128)
                  nb_ = 0
                  for m in range(8):
                      wsl = wd.get()
                      for qq in range(2):
                          cs = slice(qq * G, (qq + 1) * G)
                          bank = pb[nb_ % 4]
                          nb_ += 1
                          for k in range(NF):
                              MM(bank.ap[:, 0:G], wsl.ap[:, k, :], actT.ap[:, k, cs], k == 0, k == NF - 1, [wsl.b, actT.b], [bank.b])
                          ACT(ybuf2[qq].ap[:, m, :], bank.ap[:, 0:G], AF.Copy, [bank.b], [ybuf2[qq].b])
                  chk("ffn_dn")
                  for qq in range(2):
                      Q = 2 * hf + qq
                      resid_update(xgs[qq], ybuf2[qq], PV_POSTFFN, sqs[qq], rstds[qq], tmps[qq], pb[6 + qq])
                      P.dma("sp", xview(Xout, Q), xgs[qq].ap, reads=[xgs[qq].b], writes=[bXout[Q]], key="st")
                      chk(f"ffn_q{Q}")
                  P.barrier()
              xin_bufs = bXout

        except StopBuild as ex:
            print("[kernel] build stopped at", ex)
        nw, ni = P.emit(final_wait_keys=[k for k in ["st"] if k in P.dma_count])
        print(f"[kernel] instrs={ni} waits={nw} arena_peak_words={A.peak}")
    return nc


def _consts():
    s = np.arange(128)[:, None]
    t = np.arange(128)[None, :]
    ident = np.eye(128, dtype=np.float32)
    tri_neg = -(s >= t).astype(np.float32)
    ones_neg = -np.ones((128, 128), np.float32)
    ones = np.ones((128, 128), np.float32)
    mask_ml = (s <= t).astype(np.float32)
    c = np.arange(512)[None, :]
    sbmask = (c > s).astype(np.float32)
    cbf = np.concatenate([ident, tri_neg, ones_neg, ones, mask_ml, sbmask], axis=1)
    wins = np.array([2, 4, 8, 16], np.float32)
    corr = np.zeros((128, 2, 16), np.float32)
    for b in range(2):
        for half in range(2):
            w = wins[2 * b + half]
            tt = np.arange(16, dtype=np.float32)
            corr[half * 64:(half + 1) * 64, b, :] = w / np.minimum(tt + 1.0, w)
    cf32 = np.concatenate([mask_ml, ones, corr.reshape(128, 32)], axis=1)
    return np.ascontiguousarray(cbf), np.ascontiguousarray(cf32)


def _fm(v):
    return np.ascontiguousarray(v.reshape(-1, 128).T)


def _blk(w, c0, nblk):
    K = w.shape[0]
    kc = K // 128
    sub = w[:, c0:c0 + nblk * 128].reshape(kc, 128, nblk, 128)
    return np.ascontiguousarray(sub.transpose(2, 1, 0, 3).reshape(nblk, 128, kc * 128))


def _prep_layer(inp, l):
    w_in = inp["w_in"][l]
    fm = np.concatenate([_blk(w_in, 0, 4), _blk(w_in, 512, 4), _blk(w_in, 2056, 2), _blk(w_in, 2312, 2), _blk(w_in, 2824, 2)], axis=0)
    tmc = np.concatenate([w_in[:, 1024:1536], w_in[:, 1536:2048], w_in[:, 2568:2824], w_in[:, 2048:2056]], axis=1)
    tm = np.ascontiguousarray(tmc.reshape(KC, 128, 1288).transpose(1, 0, 2).reshape(128, KC * 1288))
    wout = _blk(inp["w_out"][l], 0, 8)
    wup = _blk(inp["ffn_w_up"][l], 0, 2 * NF)
    wdn = _blk(inp["ffn_w_down"][l], 0, 8)
    pvec = np.zeros((128, NPV), np.float32)
    pvec[:, PV_PREMIX:PV_PREMIX + 8] = _fm(inp["pre_mix_g"][l])
    pvec[:, PV_POSTMIX:PV_POSTMIX + 8] = _fm(inp["post_mix_g"][l])
    pvec[:, PV_PREFFN:PV_PREFFN + 8] = _fm(inp["pre_ffn_g"][l])
    pvec[:, PV_POSTFFN:PV_POSTFFN + 8] = _fm(inp["post_ffn_g"][l])
    pvec[:, PV_MIXG:PV_MIXG + 8] = _fm(np.concatenate([inp["mlstm_out_g"][l], inp["sb_out_g"][l], inp["pool_out_g"][l]]))
    cwm = inp["mlstm_conv_w"][l]
    pvec[:, PV_CWM:PV_CWM + 32] = cwm.reshape(4, 8, 128).transpose(2, 1, 0).reshape(128, 32)
    pvec[:, PV_CBM:PV_CBM + 8] = _fm(inp["mlstm_conv_b"][l])
    cwf = inp["ffn_conv_w"][l]
    pvec[:, PV_CWF:PV_CWF + 132] = cwf.reshape(3, 44, 128).transpose(2, 1, 0).reshape(128, 132)
    pvec[:, PV_CBF:PV_CBF + 44] = _fm(inp["ffn_conv_b"][l])
    pvec[:, PV_PSCALE:PV_PSCALE + 2] = _fm(inp["pool_scale"][l])
    invw = np.zeros((128, 2), np.float32)
    invw[0:64, 0], invw[64:128, 0], invw[0:64, 1], invw[64:128, 1] = 0.5, 0.25, 0.125, 0.0625
    pvec[:, PV_INVW:PV_INVW + 2] = invw
    gb = np.concatenate([inp["i_bias"][l], inp["f_bias"][l]]).astype(np.float32)
    gbias = np.ascontiguousarray(np.broadcast_to(np.tile(gb, 4)[None, :], (128, 32)))
    pw = inp["pool_w"][l]
    poolw = np.zeros((128, 2, 128), np.float32)
    for b in range(2):
        poolw[0:64, b, 0:64] = pw[2 * b]
        poolw[64:128, b, 64:128] = pw[2 * b + 1]
    return {f"wfm{l}": fm, f"wtm{l}": tm, f"wout{l}": wout, f"wup{l}": wup, f"wdn{l}": wdn,
            f"pvec{l}": pvec, f"gbias{l}": gbias, f"poolw{l}": np.ascontiguousarray(poolw.reshape(128, 256))}


FUSED = True


def kernel(**inputs):
    inp = {k: np.asarray(v, dtype=np.float32) for k, v in inputs.items()}
    x = inp["x"]
    cbf, cf32 = _consts()
    lay = [_prep_layer(inp, l) for l in range(2)]
    xT = [np.ascontiguousarray(x[b].T) for b in range(NCORES)]
    if FUSED:
        nc = build_program([0, 1])
        maps = []
        for b in range(NCORES):
            m = {"xT": xT[b], "cbf": cbf, "cf32": cf32}
            m.update(lay[0])
            m.update(lay[1])
            maps.append(m)
        res = run_bass_kernel_spmd(nc, maps, core_ids=list(range(NCORES)))
        yT = [res.results[b]["yT"] for b in range(NCORES)]
    else:
        cur = xT
        for l in range(2):
            nc = build_program([l])
            maps = []
            for b in range(NCORES):
                m = {"xT": cur[b], "cbf": cbf, "cf32": cf32}
                m.update(lay[l])
                maps.append(m)
            res = run_bass_kernel_spmd(nc, maps, core_ids=list(range(NCORES)))
            cur = [np.ascontiguousarray(res.results[b]["yT"]) for b in range(NCORES)]
        yT = cur
    out = np.stack([np.asarray(yT[b]).T for b in range(NCORES)], axis=0)
    return np.ascontiguousarray(out.astype(np.float32))
```

```python
import math
import numpy as np
from contextlib import ExitStack
import concourse.bass as bass
import concourse.mybir as mybir
from concourse.bass_utils import run_bass_kernel_spmd

F32 = mybir.dt.float32
BF16 = mybir.dt.bfloat16
AF = mybir.ActivationFunctionType
ALU = mybir.AluOpType

S = 2048
D = 1024
KC = 8
G = 512
NGR = S // G
DFF = 2816
NF = DFF // 128
DIN = 3080
EPS = 1e-6
NCORES = 8
ENGS = ("pe", "act", "dve", "pool", "sp")
MAX_SWDGE = 3

PV_PREMIX, PV_POSTMIX, PV_PREFFN, PV_POSTFFN, PV_MIXG = 0, 8, 16, 24, 32
PV_CWM, PV_CBM, PV_CWF, PV_CBF, PV_PSCALE, PV_INVW = 40, 72, 80, 212, 256, 258
NPV = 260


class Buf:
    __slots__ = ("name", "writer", "readers")

    def __init__(self, name):
        self.name = name
        self.writer = None
        self.readers = []


class Instr:
    __slots__ = ("eng", "fn", "deps", "dwaits", "signal", "sigval", "dma_key", "finish")

    def __init__(self, eng, fn, dma_key=None):
        self.eng = eng
        self.fn = fn
        self.deps = []
        self.dwaits = {}
        self.signal = False
        self.sigval = 0
        self.dma_key = dma_key
        self.finish = 0.0


class Prog:
    def __init__(self, nc, stack):
        self.nc = nc
        self.stack = stack
        self.E = {"pe": nc.tensor, "act": nc.scalar, "dve": nc.vector, "pool": nc.gpsimd, "sp": nc.sync}
        self.instrs = []
        self.dma_count = {}
        self.sems = {}
        self.last = {e: None for e in ENGS}
        self.efree = {e: 0.0 for e in ENGS}
        self.pool_dmas = []
        self.defer = None
        self.bar = {e: None for e in ENGS}

    def _sem(self, key):
        if key not in self.sems:
            self.sems[key] = self.stack.enter_context(self.nc.semaphore("s_" + str(key)))
        return self.sems[key]

    def barrier(self):
        tmax = max(self.efree.values())
        for e in ENGS:
            self.efree[e] = tmax
        lasts = [i for i in self.last.values() if i is not None]
        snap = dict(self.dma_count)
        for e in ENGS:
            self.bar[e] = (lasts, snap)

    def _producers(self, eng, reads, writes, dma_key):
        out = []
        strict = eng != "pe"
        for b in reads:
            if b.writer is not None:
                out.append(b.writer)
        for b in writes:
            w = b.writer
            if w is not None and (w.dma_key is not None or w.eng != eng or dma_key is not None or strict):
                out.append(w)
            for r in b.readers:
                if r.dma_key is not None or r.eng != eng or dma_key is not None or strict:
                    out.append(r)
        return out

    def peek_start(self, desc):
        eng, fn, reads, writes, dma_key, dur = desc[:6]
        t = self.efree[eng]
        for p in self._producers(eng, reads, writes, dma_key):
            t = max(t, p.finish + (0.45 if p.eng != eng or p.dma_key is not None else 0.1))
        return t

    def op(self, eng, fn, reads=(), writes=(), dma_key=None, dur=0.5, atomic=False):
        if self.defer is not None:
            self.defer.append((eng, fn, tuple(reads), tuple(writes), dma_key, dur, atomic))
            return None
        return self._op(eng, fn, reads, writes, dma_key, dur)

    def _op(self, eng, fn, reads=(), writes=(), dma_key=None, dur=0.5):
        start = self.peek_start((eng, fn, reads, writes, dma_key, dur))
        ins = Instr(eng, fn, dma_key)
        if dma_key is not None:
            self.efree[eng] = start + 0.6
            ins.finish = start + 3.5
        else:
            ins.finish = start + dur
            self.efree[eng] = ins.finish
        deps = {}

        def add(p):
            if p is not None:
                deps[id(p)] = p

        for b in reads:
            add(b.writer)
        strict = eng != "pe"
        for b in writes:
            w = b.writer
            if w is not None and (w.dma_key is not None or w.eng != eng or dma_key is not None or strict):
                add(w)
            for r in b.readers:
                if r.dma_key is not None or r.eng != eng or dma_key is not None or strict:
                    add(r)
        if self.bar[eng] is not None:
            lasts, snap = self.bar[eng]
            self.bar[eng] = None
            for p in lasts:
                if p.eng != eng or dma_key is not None or eng == "pool":
                    add(p)
            for k, c in snap.items():
                ins.dwaits[k] = max(ins.dwaits.get(k, 0), c)
        for p in deps.values():
            if p.dma_key is not None:
                ins.dwaits[p.dma_key] = max(ins.dwaits.get(p.dma_key, 0), self.dma_count[p.dma_key])
            else:
                p.signal = True
                ins.deps.append(p)
        if dma_key is not None:
            self.dma_count[dma_key] = self.dma_count.get(dma_key, 0) + 1
            if eng == "pool":
                if len(self.pool_dmas) >= MAX_SWDGE:
                    k_, c_ = self.pool_dmas[-MAX_SWDGE]
                    ins.dwaits[k_] = max(ins.dwaits.get(k_, 0), c_)
                self.pool_dmas.append((dma_key, self.dma_count[dma_key]))
        else:
            self.last[eng] = ins
        for b in writes:
            b.writer = ins
            b.readers = []
        for b in reads:
            if dma_key is None:
                b.readers = [r for r in b.readers if r.dma_key is not None or r.eng != eng]
            b.readers.append(ins)
        self.instrs.append(ins)
        return ins

    def record(self, fn):
        assert self.defer is None
        self.defer = []
        try:
            fn()
            return self.defer
        finally:
            self.defer = None

    def run_streams(self, streams, after=None):
        after = after or {}
        pos = [0] * len(streams)
        while True:
            best = None
            for i, st in enumerate(streams):
                if pos[i] >= len(st):
                    continue
                if any(pos[j] < len(streams[j]) for j in after.get(i, ())):
                    continue
                t = self.peek_start(st[pos[i]])
                if best is None or t < best[0]:
                    best = (t, i)
            if best is None:
                break
            i = best[1]
            while True:
                d_ = streams[i][pos[i]]
                self._op(*d_[:6])
                pos[i] += 1
                if not d_[6] or pos[i] >= len(streams[i]):
                    break

    def dma(self, queue, out, in_, reads=(), writes=(), key="dma", **kw):
        return self.op(queue, lambda e: e.dma_start(out=out, in_=in_, **kw), reads, writes, dma_key=key, dur=3.5)

    def emit(self, final_wait_keys=(), final_eng="sp"):
        sigcount = {e: 0 for e in ENGS}
        waited = {e: {} for e in ENGS}
        for ins in self.instrs:
            if ins.dma_key is None and ins.signal:
                sigcount[ins.eng] += 1
                ins.sigval = sigcount[ins.eng]
        nw = 0
        for ins in self.instrs:
            eng = self.E[ins.eng]
            need = {}
            for p in ins.deps:
                k = "eng_" + p.eng
                need[k] = max(need.get(k, 0), p.sigval)
            for dk, c in ins.dwaits.items():
                k = "dma_" + str(dk)
                need[k] = max(need.get(k, 0), 16 * c)
            for k, v in need.items():
                if waited[ins.eng].get(k, 0) >= v:
                    continue
                eng.wait_ge(self._sem(k), v)
                waited[ins.eng][k] = v
                nw += 1
            bi = ins.fn(eng)
            if ins.dma_key is not None:
                bi.then_inc(self._sem("dma_" + str(ins.dma_key)), 16)
            elif ins.signal:
                bi.then_inc(self._sem("eng_" + ins.eng), 1)
        print("[kernel] sigcount", sigcount, "dma", {k: 16 * v for k, v in self.dma_count.items()})
        eng = self.E[final_eng]
        for key in final_wait_keys:
            eng.wait_ge(self._sem("dma_" + str(key)), 16 * self.dma_count[key])
        return nw, len(self.instrs)


class Arena:
    def __init__(self, ap, nwords):
        self.ap = ap
        self.n = nwords
        self.top = 0
        self.peak = 0

    def alloc(self, dtype, shape):
        ne = 1
        for d in shape:
            ne *= d
        nbytes = ne * (2 if dtype == BF16 else 4)
        words = (nbytes + 3) // 4
        words = (words + 7) // 8 * 8
        assert self.top + words <= self.n, f"arena overflow {self.top + words} > {self.n}"
        v = self.ap[:, self.top:self.top + words]
        self.top += words
        self.peak = max(self.peak, self.top)
        if dtype == BF16:
            v = v.bitcast(BF16)
        v = v[:, 0:ne]
        if len(shape) == 2:
            v = v.rearrange("p (a b) -> p a b", a=shape[0])
        elif len(shape) == 3:
            v = v.rearrange("p (a b c) -> p a b c", a=shape[0], b=shape[1])
        return v


class StopBuild(Exception):
    pass


STOP_AT = None
HF_RANGE = (0, 1)
ML_INTERLEAVE = True
SUBBANK = 1
ML_SKIP = False
ML_LIMIT = 0


def chk(name):
    if STOP_AT == name:
        raise StopBuild(name)


class TB:
    def __init__(self, ap, name):
        self.ap = ap
        self.b = Buf(name)
        self.h = None


def build_program(layers, nlayers_total=2, debug=False):
    nc = bass.Bass("TRN2", target_bir_lowering=False)
    dr = {}

    def din(name, shape):
        dr[name] = nc.dram_tensor(name, shape, F32, kind="ExternalInput").ap()
        return dr[name]

    xT_in = din("xT", [D, S])
    for l in layers:
        din(f"wfm{l}", [14, 128, KC * 128])
        din(f"wtm{l}", [128, KC * 1288])
        din(f"wout{l}", [8, 128, KC * 128])
        din(f"wup{l}", [2 * NF, 128, KC * 128])
        din(f"wdn{l}", [8, 128, NF * 128])
        din(f"pvec{l}", [128, NPV])
        din(f"gbias{l}", [128, 32])
        din(f"poolw{l}", [128, 256])
    din("cbf", [128, 1152])
    din("cf32", [128, 288])
    yT = nc.dram_tensor("yT", [D, S], F32, kind="ExternalOutput").ap()
    if debug:
        dbg_mix = nc.dram_tensor("dbg_mix", [D, S], F32, kind="ExternalOutput").ap()
        dbg_mid = nc.dram_tensor("dbg_mid", [D, S], F32, kind="ExternalOutput").ap()
        dbg_act = nc.dram_tensor("dbg_act", [DFF, S], F32, kind="ExternalOutput").ap()
        dbg_h2 = nc.dram_tensor("dbg_h2", [D, S], F32, kind="ExternalOutput").ap()
    xa = nc.dram_tensor("xa", [D, S], F32, kind="Internal").ap()
    xb = nc.dram_tensor("xb", [D, S], F32, kind="Internal").ap()

    with ExitStack() as st:
        P = Prog(nc, st)
        NW = 48640
        arena_t = st.enter_context(nc.sbuf_tensor("arena", [128, NW], F32))
        A = Arena(arena_t, NW)
        pbt = [st.enter_context(nc.psum_tensor(f"pb{i}", [128, 512], F32)) for i in range(8)]
        pb = [TB(t, f"pb{i}") for i, t in enumerate(pbt)]

        def new(dtype, shape, name):
            return TB(A.alloc(dtype, shape), name)

        cbf = new(BF16, [1152], "cbf")
        cf32 = new(F32, [288], "cf32")
        ident = cbf.ap[:, 0:128]
        tri_neg = cbf.ap[:, 128:256]
        ones_neg = cbf.ap[:, 256:384]
        ones_bf = cbf.ap[:, 384:512]
        mask_ml = cbf.ap[:, 512:640]
        sbmask = cbf.ap[:, 640:1152]
        tri_f = cf32.ap[:, 0:128]
        ones_f = cf32.ap[:, 128:256]
        poolcorr = cf32.ap[:, 256:288].rearrange("p (b t) -> p b t", b=2)
        pvec = new(F32, [NPV], "pvec")
        gbias = new(F32, [32], "gbias")
        poolw = new(BF16, [2, 128], "poolw")
        Cst = new(F32, [4, 132], "Cst")
        Cbf = new(BF16, [4, 132], "Cbf")
        hal_m = new(F32, [8, 4], "hal_m")
        hal_f = new(F32, [2 * NF, 2], "hal_f")
        hal_f_b = [Buf(f"hal_f{i}") for i in range(4)]
        hal_p = new(F32, [2, 16], "hal_p")
        k_sb = new(BF16, [2, S], "k_sb")
        v_sb = new(BF16, [S // 128, 256], "v_sb")
        pv = pvec.ap

        P.dma("pool", cbf.ap, dr["cbf"], writes=[cbf.b], key="const")
        P.dma("sp", cf32.ap, dr["cf32"], writes=[cf32.b], key="constf")

        def _fs(ap):
            n = 1
            for d_ in list(ap.shape)[1:]:
                n *= int(d_)
            return n

        def ACT(out, in_, func, reads, writes, scale=1.0, bias=0.0, accum=None):
            kw = {}
            if accum is not None:
                kw["accum_out"] = accum
            du = 0.2 + 0.00095 * _fs(out)
            if func == AF.Copy:
                return P.op("act", lambda e: e.activation(out=out, in_=in_, func=func, scale=scale, **kw), reads, writes, dur=du)
            return P.op("act", lambda e: e.activation(out=out, in_=in_, func=func, scale=scale, bias=bias, **kw), reads, writes, dur=du)

        def TT(eng, out, a, b, op, reads, writes):
            du = (0.12 + 0.0011 * _fs(out)) if eng == "dve" else (0.3 + 0.002 * _fs(out))
            return P.op(eng, lambda e: e.tensor_tensor(out=out, in0=a, in1=b, op=op), reads, writes, dur=du)

        def TS(eng, out, a, s1, s2, op0, op1, reads, writes):
            du = 0.12 + 0.0011 * _fs(out)
            if s2 is None:
                return P.op(eng, lambda e: e.tensor_scalar(out=out, in0=a, scalar1=s1, scalar2=None, op0=op0), reads, writes, dur=du)
            return P.op(eng, lambda e: e.tensor_scalar(out=out, in0=a, scalar1=s1, scalar2=s2, op0=op0, op1=op1), reads, writes, dur=du)

        def STT(out, a, s, b, op0, op1, reads, writes):
            return P.op("dve", lambda e: e.scalar_tensor_tensor(out=out, in0=a, scalar=s, in1=b, op0=op0, op1=op1), reads, writes,
                        dur=0.12 + 0.0012 * _fs(out))

        def CP(eng, out, in_, reads, writes):
            du = (0.1 + 0.0009 * _fs(out)) if eng == "dve" else (0.3 + 0.002 * _fs(out))
            return P.op(eng, lambda e: e.tensor_copy(out=out, in_=in_), reads, writes, dur=du)

        def MS(eng, out, val, writes):
            return P.op(eng, lambda e: e.memset(out, val), [], writes, dur=0.3)

        def MM(out, lhsT, rhs, start, stop, reads, writes, sgc=False):
            return P.op("pe", lambda e: e.matmul(out, lhsT=lhsT, rhs=rhs, start=start, stop=stop, skip_group_check=sgc), reads, writes,
                        dur=max(_fs(out), 64) / 2400.0 * 1.3, atomic=(not stop) and (not sgc))

        def TR(out, in_, reads, writes):
            return P.op("pe", lambda e: e.transpose(out, in_, ident), list(reads) + [cbf.b], writes, dur=0.15)

        def xview(ap, q):
            return ap.rearrange("(k p) t -> p k t", p=128)[:, :, q * G:(q + 1) * G]

        class WStream:
            def __init__(self, slots, keybase, hold=1):
                self.hold = hold
                self.slots = slots
                self.keybase = keybase
                self.tasks = []
                self.issued = 0
                self.used = 0

            def add(self, dram_ap, ncols):
                self.tasks.append((dram_ap, ncols))

            def _issue(self):
                i = self.issued
                src, ncols = self.tasks[i]
                s = self.slots[i % len(self.slots)]
                dst = s.ap.rearrange("p a b -> p (a b)")[:, 0:ncols] if len(s.ap.shape) == 3 else s.ap[:, 0:ncols]
                P.dma("pool", dst, src, writes=[s.b], key=f"{self.keybase}{i % len(self.slots)}", max_dma_last_dim=4096)
                self.issued += 1

            def get(self):
                while self.issued < len(self.tasks) and self.issued < self.used + len(self.slots) - (self.hold - 1):
                    self._issue()
                s = self.slots[self.used % len(self.slots)]
                self.used += 1
                return s

        def rstd_from_ssq(ps, width, dst, reads, n=G):
            ACT(dst.ap[:, 0:n], ps, AF.Ln, reads, [dst.b], scale=1.0 / width, bias=EPS)
            ACT(dst.ap[:, 0:n], dst.ap[:, 0:n], AF.Exp, [dst.b], [dst.b], scale=-0.5)

        def norm_fm_parts(src, nblk, width, gcol, dst_ap, dst_b, sq, rstd, bank):
            def p1():
                ACT(sq.ap[:, 0:nblk, :], src.ap[:, 0:nblk, :], AF.Square, [src.b], [sq.b])

            def p2():
                for j in range(nblk):
                    MM(bank.ap[:, 0:G], ones_bf, sq.ap[:, j, :], j == 0, j == nblk - 1, [sq.b, cbf.b], [bank.b])

            def p3():
                rstd_from_ssq(bank.ap[:, 0:G], width, rstd, [bank.b])

            def p4():
                for j in range(nblk):
                    STT(dst_ap[:, j, :], src.ap[:, j, :], pv[:, gcol + j:gcol + j + 1], rstd.ap[:, 0:G], ALU.mult, ALU.mult,
                        [src.b, rstd.b, pvec.b], [dst_b])
            return [p1, p2, p3, p4]

        def norm_fm(*a):
            for f_ in norm_fm_parts(*a):
                f_()

        def resid_parts(xg, ybuf, gcol, sq, rstd, tmp, bank):
            def p1():
                ACT(sq.ap, ybuf.ap, AF.Square, [ybuf.b], [sq.b])

            def p2():
                for j in range(KC):
                    MM(bank.ap[:, 0:G], ones_bf, sq.ap[:, j, :], j == 0, j == KC - 1, [sq.b, cbf.b], [bank.b])

            def p3():
                rstd_from_ssq(bank.ap[:, 0:G], D, rstd, [bank.b])

            def p4(j0, j1):
                for j in range(j0, j1):
                    t = tmp[j % 2]
                    STT(t.ap, ybuf.ap[:, j, :], pv[:, gcol + j:gcol + j + 1], rstd.ap[:, 0:G], ALU.mult, ALU.mult,
                        [ybuf.b, rstd.b, pvec.b], [t.b])
                    TT("dve", xg.ap[:, j, :], xg.ap[:, j, :], t.ap, ALU.add, [xg.b, t.b], [xg.b])
            return [p1, p2, p3, lambda: p4(0, 4), lambda: p4(4, 8)]

        def resid_update(*a):
            for f_ in resid_parts(*a):
                f_()

        persist_top = A.top
        try:
          for li, l in enumerate(layers):
              gl = l
              if li == 0:
                  Xin = xT_in
              else:
                  Xin = xb
              Xout = yT if li == len(layers) - 1 else xb
              Xmid = xa
              bXin = xin_bufs if li > 0 else [Buf(f"xin{q}") for q in range(NGR)]
              bXmid = [Buf(f"xmid{l}_{q}") for q in range(NGR)]
              bXout = [Buf(f"xout{l}_{q}") for q in range(NGR)]

              P.barrier()
              P.dma("sp", pvec.ap, dr[f"pvec{l}"], writes=[pvec.b], key="constf")
              P.dma("sp", gbias.ap, dr[f"gbias{l}"], writes=[gbias.b], key="constf")
              P.dma("pool", poolw.ap.rearrange("p a b -> p (a b)"), dr[f"poolw{l}"], writes=[poolw.b], key="const")
              MS("pool", hal_m.ap, 0.0, [hal_m.b])
              MS("pool", hal_f.ap, 0.0, hal_f_b)
              MS("pool", Cst.ap, 0.0, [Cst.b])
              MS("pool", Cbf.ap, 0.0, [Cbf.b])

              A.top = persist_top
              qk_m = new(BF16, [8, G], "qk_m")
              qzs = [new(BF16, [4, G], f"qz{i}") for i in range(2)]
              up = new(F32, [2, 16 + G], "up")
              v_aug = new(BF16, [4, 4, 130], "v_aug")
              sig_o = new(BF16, [4, 512], "sig_o")
              gpre = new(F32, [4, 8], "gpre")
              mixTs = [new(BF16, [8, G], f"mixT{i}") for i in range(2)]
              xg = new(F32, [8, G], "xg")
              sq = new(BF16, [8, G], "sq")
              rstd = new(F32, [G], "rstd")
              hT = new(BF16, [8, G], "hT")
              wfm = [new(BF16, [8, 128], f"wfm{i}") for i in range(3)]
              wtm = [new(BF16, [8, 264], f"wtm{i}") for i in range(2)]
              raw = [new(F32, [G + 4], f"raw{i}") for i in range(2)]
              acc = [new(F32, [G], f"acc{i}") for i in range(2)]
              for t_ in raw:
                  t_.h = Buf("rawh")
              xg3 = xg
              ybuf = new(F32, [8, G], "ybuf")
              tmp = [new(F32, [G], f"tmp{i}") for i in range(2)]
              gsp = new(F32, [4, 4], "gsp")
              gu = new(F32, [4, 4], "gu")
              gw = new(F32, [4, 4], "gw")
              ge = new(F32, [4, 4], "ge")
              mST = [new(BF16, [128], f"mST{i}") for i in range(2)]
              ktok = [new(BF16, [128], f"ktok{i}") for i in range(2)]
              vp = [new(BF16, [130], f"vp{i}") for i in range(2)]
              dn = [new(F32, [4], f"dn{i}") for i in range(2)]
              Ct = [new(F32, [132], f"Ct{i}") for i in range(2)]
              ym = [new(F32, [512], f"ym{i}") for i in range(2)]
              ymn = [new(BF16, [512], f"ymn{i}") for i in range(2)]
              junk = new(BF16, [512], "junk")
              ss = [new(F32, [2], f"ss{i}") for i in range(2)]
              ebuf = [new(F32, [G], f"e{i}") for i in range(2)]
              spb = [new(BF16, [G], f"sp{i}") for i in range(3)]
              aT = [new(BF16, [G], f"aT{i}") for i in range(3)]
              accr = [[new(BF16, [G], f"accr{h_}_{i}") for i in range(3)] for h_ in range(2)]
              ysb = new(F32, [2, G], "ysb")
              sq2 = new(BF16, [2, G], "sq2")
              rstd2 = new(F32, [G], "rstd2")
              s_a = new(F32, [16 + G], "s_a")
              s_b = new(F32, [16 + G], "s_b")
              ypT = new(BF16, [2, G], "ypT")
              ypo = new(F32, [2, G], "ypo")
              ksb_b = [Buf(f"ksb{i}") for i in range(NGR)]
              vsb_b = [Buf(f"vsb{i}") for i in range(NGR)]

              def sub(bank, ap, name):
                  t_ = TB(ap, name)
                  t_.b = bank.b
                  return t_
              pG = sub(pb[7], pb[7].ap[:, 448:480], "pG")
              pST = sub(pb[6], pb[6].ap[:, 0:128], "pST")
              pKT = sub(pb[6], pb[6].ap.bitcast(BF16)[:, 256:384], "pKT")
              pTt = sub(pb[6], pb[6].ap.bitcast(BF16)[:, 512:1024], "pTt")
              pH = sub(pb[7], pb[7].ap[:, 0:129], "pH")
              pC = sub(pb[7], pb[7].ap[:, 256:385], "pC")
              wtm_d = dr[f"wtm{l}"].rearrange("p (k c) -> p k c", k=KC)
              tmcols = [(0, 256, "v", 0), (256, 256, "v", 1), (512, 256, "o", 0), (768, 256, "o", 1), (1024, 264, "s", 0)]
              sbank = [pb[4], pb[5]]

              def s1_steps(g):
                  T0 = g * G
                  qz = qzs[g % 2]
                  steps = []
                  ws = WStream(wfm, "wfm")
                  for bi in range(14):
                      ws.add(dr[f"wfm{l}"][bi], KC * 128)
                  state = {"wt_issued": 0, "nb": 0}

                  def st_norm():
                      P.dma("sp", xg.ap, xview(Xin, g), reads=[bXin[g]], writes=[xg.b], key="ldx")
                      MS("pool", qz.ap, 0.0, [qz.b])
                  steps.append(st_norm)
                  steps.extend(norm_fm_parts(xg, KC, D, PV_PREMIX, hT.ap, hT.b, sq, rstd, pb[4]))

                  def st_fm_taps(bi):
                      rb = raw[bi % 2]
                      ab = acc[bi % 2]
                      cw = PV_CWM + bi * 4
                      TS("dve", ab.ap, rb.ap[:, 3:3 + G], pv[:, cw + 3:cw + 4], pv[:, PV_CBM + bi:PV_CBM + bi + 1],
                         ALU.mult, ALU.add, [rb.b, pvec.b], [ab.b])
                      for j in (2, 1, 0):
                          STT(ab.ap, rb.ap[:, j:j + G], pv[:, cw + j:cw + j + 1], ab.ap, ALU.mult, ALU.add,
                              [rb.b, rb.h, ab.b, pvec.b], [ab.b])

                  def st_fm_silu(bi):
                      ACT(qk_m.ap[:, bi, :], acc[bi % 2].ap, AF.Silu, [acc[bi % 2].b], [qk_m.b])

                  def st_fm(bi):
                      wsl = ws.get()
                      bank = sbank[bi % 2]
                      for k in range(KC):
                          MM(bank.ap[:, 0:G], wsl.ap[:, k, :], hT.ap[:, k, :], k == 0, k == KC - 1, [wsl.b, hT.b], [bank.b])
                      if bi < 8:
                          rb = raw[bi % 2]
                          ACT(rb.ap[:, 0:3], hal_m.ap[:, bi, 0:3], AF.Copy, [hal_m.b], [rb.h])
                          ACT(rb.ap[:, 3:3 + G], bank.ap[:, 0:G], AF.Copy, [bank.b], [rb.b])
                          ACT(hal_m.ap[:, bi, 0:3], rb.ap[:, G:G + 3], AF.Copy, [rb.b], [hal_m.b])
                      elif bi < 10:
                          b = bi - 8
                          ACT(qz.ap[0:64, 2 * b, :], bank.ap[0:64, 0:G], AF.Copy, [bank.b], [qz.b], scale=0.125)
                          ACT(qz.ap[64:128, 2 * b + 1, :], bank.ap[64:128, 0:G], AF.Copy, [bank.b], [qz.b], scale=0.125)
                      elif bi < 12:
                          b = bi - 10
                          ACT(k_sb.ap[:, b, T0:T0 + G], bank.ap[:, 0:G], AF.Copy, [bank.b], [ksb_b[g]])
                      else:
                          b = bi - 12
                          if b == 0:
                              if g == 0:
                                  MS("pool", up.ap[:, :, 0:16], 0.0, [up.b])
                              else:
                                  CP("pool", up.ap[:, :, 0:16], hal_p.ap, [hal_p.b], [up.b])
                          ACT(up.ap[:, b, 16:16 + G], bank.ap[:, 0:G], AF.Copy, [bank.b], [up.b])
                  for bi in range(14):
                      steps.append((lambda bi_: (lambda: st_fm(bi_)))(bi))
                      if bi < 8:
                          steps.append((lambda bi_: (lambda: st_fm_taps(bi_)))(bi))
                          steps.append((lambda bi_: (lambda: st_fm_silu(bi_)))(bi))

                  def wt_get(ci):
                      while state["wt_issued"] < len(tmcols) and state["wt_issued"] <= ci + 1:
                          i_ = state["wt_issued"]
                          c0_, nc_, _, _ = tmcols[i_]
                          s_ = wtm[i_ % 2]
                          P.dma("pool", s_.ap[:, :, 0:nc_], wtm_d[:, :, c0_:c0_ + nc_], writes=[s_.b], key=f"wtm{i_ % 2}",
                                max_dma_last_dim=4096)
                          state["wt_issued"] += 1
                      return wtm[ci % 2]

                  def st_tm(ci, i):
                      c0, ncol, kind, half = tmcols[ci]
                      if i == 0:
                          if ci == 0:
                              MS("pool", v_aug.ap, 1.0, [v_aug.b])
                      wsl = wt_get(ci) if i == 0 else wtm[ci % 2]
                      bank = sbank[state["nb"] % 2]
                      state["nb"] += 1
                      for k in range(KC):
                          MM(bank.ap[:, 0:ncol], hT.ap[:, k, i * 128:(i + 1) * 128], wsl.ap[:, k, 0:ncol], k == 0, k == KC - 1,
                             [wsl.b, hT.b], [bank.b])
                      if kind == "v":
                          ACT(v_aug.ap[:, i, 2 * half:2 * half + 2, 0:128], bank.ap[:, 0:256].rearrange("p (h d) -> p h d", h=2), AF.Copy,
                              [bank.b], [v_aug.b])
                      elif kind == "o":
                          ACT(sig_o.ap[:, i, 256 * half:256 * half + 256], bank.ap[:, 0:256], AF.Sigmoid, [bank.b], [sig_o.b])
                      else:
                          ACT(v_sb.ap[:, T0 // 128 + i, :], bank.ap[:, 0:256], AF.Copy, [bank.b], [vsb_b[g]])
                          TT("dve", gpre.ap[:, i, :], bank.ap[:, 256:264], gbias.ap[:, 0:8], ALU.add, [bank.b, gbias.b], [gpre.b])
                  for ci in range(len(tmcols)):
                      for i in range(4):
                          steps.append((lambda ci_, i_: (lambda: st_tm(ci_, i_)))(ci, i))
                  return steps

              def s3_steps(g):
                  mixT = mixTs[g % 2]
                  steps = []
                  ws = WStream(wfm, "wfm")
                  for m in range(8):
                      ws.add(dr[f"wout{l}"][m], KC * 128)

                  def st_ld():
                      P.dma("sp", xg3.ap, xview(Xin, g), reads=[bXin[g]], writes=[xg3.b], key="ldx")
                  steps.append(st_ld)

                  def st_m(m):
                      wsl = ws.get()
                      bank = sbank[m % 2]
                      for k in range(KC):
                          MM(bank.ap[:, 0:G], wsl.ap[:, k, :], mixT.ap[:, k, :], k == 0, k == KC - 1, [wsl.b, mixT.b], [bank.b])
                      ACT(ybuf.ap[:, m, :], bank.ap[:, 0:G], AF.Copy, [bank.b], [ybuf.b])
                  for m in range(8):
                      steps.append((lambda m_: (lambda: st_m(m_)))(m))

                  steps.extend(resid_parts(xg3, ybuf, PV_POSTMIX, sq, rstd, tmp, pb[4]))

                  def st_res():
                      P.dma("sp", xview(Xmid, g), xg3.ap, reads=[xg3.b], writes=[bXmid[g]], key="st")
                      if debug and li == 0:
                          P.dma("sp", xview(dbg_mid, g), xg3.ap, reads=[xg3.b], writes=[Buf("dbgx")], key="st")
                  steps.append(st_res)
                  return steps

              def gates(g):
                  gi = gpre.ap[:, :, 0:4]
                  gf = gpre.ap[:, :, 4:8]
                  ACT(gsp.ap, gf, AF.Exp, [gpre.b], [gsp.b], scale=-1.0)
                  ACT(gsp.ap, gsp.ap, AF.Ln, [gsp.b], [gsp.b], bias=1.0)
                  gsp2 = gsp.ap.rearrange("p a b -> p (a b)")
                  MM(pG.ap[:, 0:16], tri_f, gsp2, True, True, [gsp.b, cf32.b], [pG.b])
                  MM(pG.ap[:, 16:32], ones_f, gsp2, True, True, [gsp.b, cf32.b], [pG.b])
                  cum = pG.ap[:, 0:16].rearrange("p (a b) -> p a b", a=4)
                  tot = pG.ap[:, 16:32].rearrange("p (a b) -> p a b", a=4)
                  TT("dve", gu.ap, gi, cum, ALU.add, [gpre.b, pG.b], [gu.b])
                  ACT(gu.ap, gu.ap, AF.Exp, [gu.b], [gu.b], bias=math.log(128.0 ** -0.5))
                  ACT(gw.ap, cum, AF.Exp, [pG.b], [gw.b], scale=-1.0)
                  ACT(ge.ap, tot, AF.Exp, [pG.b], [ge.b], scale=-1.0)

              def ml_steps_for(g):
                  mixT = mixTs[g % 2]

                  def ml_u1(i, h, sl):
                      ts_ = slice(i * 128, (i + 1) * 128)
                      qT = qk_m.ap[:, h, ts_]
                      kT = qk_m.ap[:, 4 + h, ts_]
                      col = slice(h, h + 1)
                      MM(pST.ap, kT, qT, True, True, [qk_m.b], [pST.b])
                      TS("dve", vp[sl].ap[:, 0:129], v_aug.ap[:, i, h, 0:129], gu.ap[:, i, col], None, ALU.mult, None,
                         [v_aug.b, gu.b], [vp[sl].b])

                  def ml_u2(i, h, sl):
                      ts_ = slice(i * 128, (i + 1) * 128)
                      qT = qk_m.ap[:, h, ts_]
                      kT = qk_m.ap[:, 4 + h, ts_]
                      TT("dve", mST[sl].ap, pST.ap, mask_ml, ALU.mult, [pST.b, cbf.b], [mST[sl].b])
                      TR(pKT.ap, kT, [qk_m.b], [pKT.b])
                      MM(pH.ap, mST[sl].ap, vp[sl].ap[:, 0:129], True, False, [mST[sl].b, vp[sl].b], [pH.b])
                      MM(pH.ap, qT, Cbf.ap[:, h, 0:129], False, True, [qk_m.b, Cbf.b], [pH.b])
                      ACT(ktok[sl].ap, pKT.ap, AF.Copy, [pKT.b], [ktok[sl].b])

                  def ml_u3(i, h, sl):
                      ymt = ym[i % 2]
                      col = slice(h, h + 1)
                      d = dn[sl]
                      ACT(d.ap[:, 0:1], pH.ap[:, 128:129], AF.Abs, [pH.b, gw.b], [d.b], scale=gw.ap[:, i, col])
                      TS("dve", d.ap[:, 1:2], d.ap[:, 0:1], 1.0, None, ALU.max, None, [d.b], [d.b])
                      P.op("dve", (lambda dd: lambda e: e.reciprocal(out=dd.ap[:, 2:3], in_=dd.ap[:, 1:2]))(d), [d.b], [d.b])
                      TT("dve", d.ap[:, 3:4], d.ap[:, 2:3], gw.ap[:, i, col], ALU.mult, [d.b, gw.b], [d.b])
                      STT(ymt.ap[:, h * 128:(h + 1) * 128], pH.ap[:, 0:128], d.ap[:, 3:4], sig_o.ap[:, i, h * 128:(h + 1) * 128],
                          ALU.mult, ALU.mult, [pH.b, d.b, sig_o.b], [ymt.b])
                      MM(pC.ap, ktok[sl].ap, vp[sl].ap[:, 0:129], True, True, [ktok[sl].b, vp[sl].b], [pC.b])

                  def ml_u4(i, h, sl):
                      col = slice(h, h + 1)
                      TT("dve", Ct[sl].ap[:, 0:129], Cst.ap[:, h, 0:129], pC.ap, ALU.add, [Cst.b, pC.b], [Ct[sl].b])
                      ACT(Cst.ap[:, h, 0:129], Ct[sl].ap[:, 0:129], AF.Copy, [Ct[sl].b, ge.b], [Cst.b], scale=ge.ap[:, i, col])
                      ACT(Cbf.ap[:, h, 0:129], Ct[sl].ap[:, 0:129], AF.Copy, [Ct[sl].b, ge.b], [Cbf.b], scale=ge.ap[:, i, col])

                  def ml_fin(i):
                      ts_ = slice(i * 128, (i + 1) * 128)
                      ymt = ym[i % 2]
                      s1 = ss[i % 2]
                      ACT(junk.ap, ymt.ap, AF.Square, [ymt.b], [junk.b, s1.b], accum=s1.ap[:, 0:1])
                      ACT(s1.ap[:, 1:2], s1.ap[:, 0:1], AF.Ln, [s1.b], [s1.b], scale=1.0 / 512, bias=EPS)
                      ACT(s1.ap[:, 1:2], s1.ap[:, 1:2], AF.Exp, [s1.b], [s1.b], scale=-0.5)
                      ACT(ymn[i % 2].ap, ymt.ap, AF.Copy, [ymt.b, s1.b], [ymn[i % 2].b], scale=s1.ap[:, 1:2])
                      for h in range(4):
                          TR(pTt.ap[:, h * 128:(h + 1) * 128], ymn[i % 2].ap[:, h * 128:(h + 1) * 128], [ymn[i % 2].b], [pTt.b])
                      for h in range(4):
                          ACT(mixT.ap[:, h, ts_], pTt.ap[:, h * 128:(h + 1) * 128], AF.Copy, [pTt.b, pvec.b], [mixT.b],
                              scale=pv[:, PV_MIXG + h:PV_MIXG + h + 1])

                  steps = []
                  cnt = 0
                  for i in range(4):
                      for h in range(4):
                          for fn__ in (ml_u1, ml_u2, ml_u3, ml_u4):
                              steps.append((lambda f_, a_: (lambda: f_(*a_)))(fn__, (i, h, cnt % 2)))
                          cnt += 1
                      steps.append((lambda i_: (lambda: ml_fin(i_)))(i))
                  return steps

              def pool_step(g):
                  mixT = mixTs[g % 2]
                  for b in range(2):
                      ub = up.ap[:, b, :]
                      W_ = 16 + G
                      TT("pool", s_a.ap[:, 1:W_], ub[:, 1:W_], ub[:, 0:W_ - 1], ALU.add, [up.b], [s_a.b])
                      if b == 0:
                          TT("pool", s_b.ap[64:128, 3:W_], s_a.ap[64:128, 3:W_], s_a.ap[64:128, 1:W_ - 2], ALU.add, [s_a.b], [s_b.b])
                      else:
                          TT("pool", s_b.ap[:, 3:W_], s_a.ap[:, 3:W_], s_a.ap[:, 1:W_ - 2], ALU.add, [s_a.b], [s_b.b])
                          TT("pool", s_a.ap[:, 7:W_], s_b.ap[:, 7:W_], s_b.ap[:, 3:W_ - 4], ALU.add, [s_b.b], [s_a.b])
                          TT("pool", s_b.ap[64:128, 15:W_], s_a.ap[64:128, 15:W_], s_a.ap[64:128, 7:W_ - 8], ALU.add, [s_a.b], [s_b.b])
                      for (src, ps_) in ((s_a, slice(0, 64)), (s_b, slice(64, 128))):
                          if g == 0:
                              TT("dve", src.ap[ps_, 16:32], src.ap[ps_, 16:32], poolcorr[ps_, b, :], ALU.mult, [src.b, cf32.b], [src.b])
                          STT(ypT.ap[ps_, b, :], src.ap[ps_, 16:W_], pv[ps_, PV_INVW + b:PV_INVW + b + 1], ub[ps_, 16:W_],
                              ALU.mult, ALU.subtract, [src.b, up.b, pvec.b], [ypT.b])
                      bank = pb[6 + b]
                      MM(bank.ap[:, 0:G], poolw.ap[:, b, :], ypT.ap[:, b, :], True, True, [poolw.b, ypT.b], [bank.b])
                      ACT(ypo.ap[:, b, :], bank.ap[:, 0:G], AF.Copy, [bank.b, pvec.b], [ypo.b], scale=pv[:, PV_PSCALE + b:PV_PSCALE + b + 1])
                  CP("pool", hal_p.ap, up.ap[:, :, G:G + 16], [up.b], [hal_p.b])
                  norm_fm(ypo, 2, 256, PV_MIXG + 6, mixT.ap[:, 6:8, :], mixT.b, sq2, rstd2, pb[7])

              def sb_loop(g):
                  mixT = mixTs[g % 2]
                  qz = qzs[g % 2]
                  nj = 4 * g + 4
                  pairs = []
                  for b in range(2):
                      for hh in range(2):
                          for idx, j in enumerate(reversed(range(nj))):
                              pairs.append((b, hh, j, idx))

                  def sb_geom(p):
                      b, hh, j, idx = pairs[p]
                      r = j - 4 * g
                      tl = 128 * r if r >= 0 else 0
                      return b, hh, j, idx, r, tl, G - tl

                  def stageA(p):
                      b, hh, j, idx, r, tl, N = sb_geom(p)
                      hd = 2 * b + hh
                      zs, s3 = p % 2, p % 3
                      ring = accr[hh]
                      if idx == 0:
                          for t_ in ring:
                              MS("pool", t_.ap, 0.0, [t_.b])
                      pZ = pb[0]
                      kTj = k_sb.ap[:, b, j * 128:(j + 1) * 128]
                      qh = qz.ap[:, hd, tl:G]
                      MM(pZ.ap[:, 0:N], kTj, qh, True, True, [ksb_b[j // 4], qz.b], [pZ.b])
                      ACT(ebuf[zs].ap[:, 0:N], pZ.ap[:, 0:N], AF.Exp, [pZ.b], [ebuf[zs].b])
                      ACT(spb[s3].ap[:, 0:N], ebuf[zs].ap[:, 0:N], AF.Ln, [ebuf[zs].b], [spb[s3].b], bias=1.0)
                      if r >= 0:
                          TT("dve", spb[s3].ap[:, 0:N], spb[s3].ap[:, 0:N], sbmask[:, 0:N], ALU.mult, [spb[s3].b, cbf.b], [spb[s3].b])
                      if j > 0:
                          src, dst = ring[idx % 3], ring[(idx + 1) % 3]
                          TT("dve", dst.ap[:, tl:G], src.ap[:, tl:G], spb[s3].ap[:, 0:N], ALU.add, [src.b, spb[s3].b], [dst.b])

                  def stageB(p):
                      b, hh, j, idx, r, tl, N = sb_geom(p)
                      hd = 2 * b + hh
                      zs, s3 = p % 2, p % 3
                      ring = accr[hh]
                      first = idx == 0
                      pZR = pb[1 + zs]
                      kTj = k_sb.ap[:, b, j * 128:(j + 1) * 128]
                      qh = qz.ap[:, hd, tl:G]
                      MM(pZR.ap[:, 0:N], kTj, qh, True, False, [ksb_b[j // 4], qz.b], [pZR.b])
                      MM(pZR.ap[:, 0:N], tri_neg, spb[s3].ap[:, 0:N], False, first, [spb[s3].b, cbf.b], [pZR.b])
                      if not first:
                          cur = ring[idx % 3]
                          MM(pZR.ap[:, 0:N], ones_neg, cur.ap[:, tl:G], False, True, [cur.b, cbf.b], [pZR.b])
                      ACT(aT[s3].ap[:, 0:N], pZR.ap[:, 0:N], AF.Exp, [pZR.b], [aT[s3].b])
                      if r >= 0:
                          TT("dve", aT[s3].ap[:, 0:N], aT[s3].ap[:, 0:N], sbmask[:, 0:N], ALU.mult, [aT[s3].b, cbf.b], [aT[s3].b])

                  def stageC(p):
                      b, hh, j, idx, r, tl, N = sb_geom(p)
                      s3 = p % 3
                      first = idx == 0
                      pO = pb[3]
                      MM(pO.ap[:, tl:G], v_sb.ap[:, j, b * 128:(b + 1) * 128], aT[s3].ap[:, 0:N], first, j == 0,
                         [vsb_b[j // 4], aT[s3].b], [pO.b], sgc=True)
                      if j == 0:
                          ps_ = slice(64 * hh, 64 * hh + 64)
                          ACT(ysb.ap[ps_, b, :], pO.ap[ps_, 0:G], AF.Copy, [pO.b], [ysb.b])

                  stageA(0)
                  for p in range(len(pairs)):
                      if p + 1 < len(pairs):
                          stageA(p + 1)
                      stageB(p)
                      if p >= 1:
                          stageC(p - 1)
                  stageC(len(pairs) - 1)

              def run_all(fns):
                  for f_ in fns:
                      f_()
              for st_ in s1_steps(0):
                  st_()
              for g in range(NGR):
                  gates(g)
                  S_sb = P.record(lambda: sb_loop(g))
                  S_a1 = P.record(lambda: run_all(s3_steps(g - 1))) if g > 0 else []
                  S_b = P.record(lambda: (pool_step(g), run_all(ml_steps_for(g))))
                  S_a2 = P.record(lambda: run_all(s1_steps(g + 1))) if g + 1 < NGR else []
                  P.run_streams([S_sb, S_a1, S_b, S_a2], after={3: (1, 2)})
                  mixT = mixTs[g % 2]
                  norm_fm(ysb, 2, 256, PV_MIXG + 4, mixT.ap[:, 4:6, :], mixT.b, sq2, rstd2, pb[0])
                  chk(f"S3_{g}")
              for st_ in s3_steps(NGR - 1):
                  st_()
              P.barrier()
              chk("mixer")
              for hf in HF_RANGE:
                  A.top = persist_top
                  hT2 = new(BF16, [8, 2 * G], "hT2")
                  actT = new(BF16, [NF, 2 * G], "actT")
                  xgs = [new(F32, [8, G], f"xgf{i}") for i in range(2)]
                  sqs = [new(BF16, [8, G], f"sqf{i}") for i in range(2)]
                  rstds = [new(F32, [G], f"rstdf{i}") for i in range(2)]
                  markf = A.top
                  wfm = [new(BF16, [8, 128], f"wu{i}") for i in range(4)]
                  NSL = 4
                  rawg = [new(F32, [G + 4], f"rawg{i}") for i in range(NSL)]
                  rawv = [new(F32, [G + 4], f"rawv{i}") for i in range(NSL)]
                  accg = [new(F32, [G], f"accg{i}") for i in range(NSL)]
                  accv = [new(F32, [G], f"accv{i}") for i in range(NSL)]
                  gel = [new(F32, [G], f"gel{i}") for i in range(NSL)]
                  for t_ in rawg + rawv:
                      t_.h = Buf("rwh")
                  for qq in range(2):
                      Q = 2 * hf + qq
                      P.dma("sp", xgs[qq].ap, xview(Xmid, Q), reads=[bXmid[Q]], writes=[xgs[qq].b], key=f"ldxf{qq}")
                  for qq in range(2):
                      norm_fm(xgs[qq], KC, D, PV_PREFFN, hT2.ap[:, :, qq * G:(qq + 1) * G], hT2.b, sqs[qq], rstds[qq], pb[6 + qq])
                  chk("ffn_norm")
                  ws = WStream(wfm, "wfm", hold=2)
                  for f in range(NF):
                      ws.add(dr[f"wup{l}"][f], KC * 128)
                      ws.add(dr[f"wup{l}"][NF + f], KC * 128)
                  items = []

                  def ffn_front(f, qq, wg, wv, n):
                      Q = 2 * hf + qq
                      sl = n % NSL
                      bsl = n % 4
                      pg, pvb = pb[2 * bsl], pb[2 * bsl + 1]
                      cs = slice(qq * G, (qq + 1) * G)
                      for k in range(KC):
                          MM(pg.ap[:, 0:G], wg.ap[:, k, :], hT2.ap[:, k, cs], k == 0, k == KC - 1, [wg.b, hT2.b], [pg.b])
                      for k in range(KC):
                          MM(pvb.ap[:, 0:G], wv.ap[:, k, :], hT2.ap[:, k, cs], k == 0, k == KC - 1, [wv.b, hT2.b], [pvb.b])
                      for (fb, bank, rw, ac) in ((f, pg, rawg[sl], accg[sl]), (NF + f, pvb, rawv[sl], accv[sl])):
                          cw = PV_CWF + fb * 3
                          hb = hal_f_b[fb % 4]
                          ACT(rw.ap[:, 0:2], hal_f.ap[:, fb, :], AF.Copy, [hb], [rw.h])
                          ACT(rw.ap[:, 2:2 + G], bank.ap[:, 0:G], AF.Copy, [bank.b], [rw.b])
                          ACT(hal_f.ap[:, fb, :], rw.ap[:, G:G + 2], AF.Copy, [rw.b], [hb])
                          ACT(ac.ap, bank.ap[:, 0:G], AF.Identity, [bank.b, pvec.b], [ac.b], scale=pv[:, cw + 2:cw + 3],
                              bias=pv[:, PV_CBF + fb:PV_CBF + fb + 1])
                          for j in (1, 0):
                              STT(ac.ap, rw.ap[:, j:j + G], pv[:, cw + j:cw + j + 1], ac.ap, ALU.mult, ALU.add,
                                  [rw.b, rw.h, ac.b, pvec.b], [ac.b])

                  def ffn_back(f, qq, n):
                      sl = n % NSL
                      cs = slice(qq * G, (qq + 1) * G)
                      ACT(gel[sl].ap, accg[sl].ap, AF.Gelu_apprx_tanh, [accg[sl].b], [gel[sl].b])
                      TT("dve", actT.ap[:, f, cs], gel[sl].ap, accv[sl].ap, ALU.mult, [gel[sl].b, accv[sl].b], [actT.b])

                  cnt = 0
                  prev = None
                  for f in range(NF):
                      wg = ws.get()
                      wv = ws.get()
                      for qq in range(2):
                          ffn_front(f, qq, wg, wv, cnt)
                          if prev is not None:
                              ffn_back(*prev)
                          prev = (f, qq, cnt)
                          cnt += 1
                      chk(f"ffn_f{f}")
                  ffn_back(*prev)
                  if debug and li == 0:
                      P.dma("pool", dbg_act.rearrange("(k p) t -> p k t", p=128)[:, :, hf * 2 * G:(hf + 1) * 2 * G], actT.ap,
                            reads=[actT.b], writes=[Buf("dbga")], key="st")
                      P.dma("pool", dbg_h2.rearrange("(k p) t -> p k t", p=128)[:, :, hf * 2 * G:(hf + 1) * 2 * G], hT2.ap,
                            reads=[hT2.b], writes=[Buf("dbgh")], key="st")
                  chk("ffn_up")
                  P.barrier()
                  A.top = markf
                  ybuf2 = [new(F32, [8, G], f"ybuff{i}") for i in range(2)]
                  tmps = [[new(F32, [G], f"tmpf{q_}_{i}") for i in range(2)] for q_ in range(2)]
                  wdn = [new(BF16, [NF, 128], f"wd{i}") for i in range(2)]
                  wd = WStream(wdn, "wdn")
                  for m in range(8):
                      wd.add(dr[f"wdn{l}"][m], NF * 128)
                  nb_ = 0
                  for m in range(8):
                      wsl = wd.get()
                      for qq in range(2):
                          cs = slice(qq * G, (qq + 1) * G)
                          bank = pb[nb_ % 4]
                          nb_ += 1
                          for k in range(NF):
                              MM(bank.ap[:, 0:G], wsl.ap[:, k, :], actT.ap[:, k, cs], k == 0, k == NF - 1, [wsl.b, actT.b], [bank.b])
                          ACT(ybuf2[qq].ap[:, m, :], bank.ap[:, 0:G], AF.Copy, [bank.b], [ybuf2[qq].b])
                  chk("ffn_dn")
                  for qq in range(2):
                      Q = 2 * hf + qq
                      resid_update(xgs[qq], ybuf2[qq], PV_POSTFFN, sqs[qq], rstds[qq], tmps[qq], pb[6 + qq])
                      P.dma("sp", xview(Xout, Q), xgs[qq].ap, reads=[xgs[qq].b], writes=[bXout[Q]], key="st")
                      chk(f"ffn_q{Q}")
                  P.barrier()
              xin_bufs = bXout

        except StopBuild as ex:
            print("[kernel] build stopped at", ex)
        nw, ni = P.emit(final_wait_keys=[k for k in ["st"] if k in P.dma_count])
        print(f"[kernel] instrs={ni} waits={nw} arena_peak_words={A.peak}")
    return nc


def _consts():
    s = np.arange(128)[:, None]
    t = np.arange(128)[None, :]
    ident = np.eye(128, dtype=np.float32)
    tri_neg = -(s >= t).astype(np.float32)
    ones_neg = -np.ones((128, 128), np.float32)
    ones = np.ones((128, 128), np.float32)
    mask_ml = (s <= t).astype(np.float32)
    c = np.arange(512)[None, :]
    sbmask = (c > s).astype(np.float32)
    cbf = np.concatenate([ident, tri_neg, ones_neg, ones, mask_ml, sbmask], axis=1)
    wins = np.array([2, 4, 8, 16], np.float32)
    corr = np.zeros((128, 2, 16), np.float32)
    for b in range(2):
        for half in range(2):
            w = wins[2 * b + half]
            tt = np.arange(16, dtype=np.float32)
            corr[half * 64:(half + 1) * 64, b, :] = w / np.minimum(tt + 1.0, w)
    cf32 = np.concatenate([mask_ml, ones, corr.reshape(128, 32)], axis=1)
    return np.ascontiguousarray(cbf), np.ascontiguousarray(cf32)


def _fm(v):
    return np.ascontiguousarray(v.reshape(-1, 128).T)


def _blk(w, c0, nblk):
    K = w.shape[0]
    kc = K // 128
    sub = w[:, c0:c0 + nblk * 128].reshape(kc, 128, nblk, 128)
    return np.ascontiguousarray(sub.transpose(2, 1, 0, 3).reshape(nblk, 128, kc * 128))


def _prep_layer(inp, l):
    w_in = inp["w_in"][l]
    fm = np.concatenate([_blk(w_in, 0, 4), _blk(w_in, 512, 4), _blk(w_in, 2056, 2), _blk(w_in, 2312, 2), _blk(w_in, 2824, 2)], axis=0)
    tmc = np.concatenate([w_in[:, 1024:1536], w_in[:, 1536:2048], w_in[:, 2568:2824], w_in[:, 2048:2056]], axis=1)
    tm = np.ascontiguousarray(tmc.reshape(KC, 128, 1288).transpose(1, 0, 2).reshape(128, KC * 1288))
    wout = _blk(inp["w_out"][l], 0, 8)
    wup = _blk(inp["ffn_w_up"][l], 0, 2 * NF)
    wdn = _blk(inp["ffn_w_down"][l], 0, 8)
    pvec = np.zeros((128, NPV), np.float32)
    pvec[:, PV_PREMIX:PV_PREMIX + 8] = _fm(inp["pre_mix_g"][l])
    pvec[:, PV_POSTMIX:PV_POSTMIX + 8] = _fm(inp["post_mix_g"][l])
    pvec[:, PV_PREFFN:PV_PREFFN + 8] = _fm(inp["pre_ffn_g"][l])
    pvec[:, PV_POSTFFN:PV_POSTFFN + 8] = _fm(inp["post_ffn_g"][l])
    pvec[:, PV_MIXG:PV_MIXG + 8] = _fm(np.concatenate([inp["mlstm_out_g"][l], inp["sb_out_g"][l], inp["pool_out_g"][l]]))
    cwm = inp["mlstm_conv_w"][l]
    pvec[:, PV_CWM:PV_CWM + 32] = cwm.reshape(4, 8, 128).transpose(2, 1, 0).reshape(128, 32)
    pvec[:, PV_CBM:PV_CBM + 8] = _fm(inp["mlstm_conv_b"][l])
    cwf = inp["ffn_conv_w"][l]
    pvec[:, PV_CWF:PV_CWF + 132] = cwf.reshape(3, 44, 128).transpose(2, 1, 0).reshape(128, 132)
    pvec[:, PV_CBF:PV_CBF + 44] = _fm(inp["ffn_conv_b"][l])
    pvec[:, PV_PSCALE:PV_PSCALE + 2] = _fm(inp["pool_scale"][l])
    invw = np.zeros((128, 2), np.float32)
    invw[0:64, 0], invw[64:128, 0], invw[0:64, 1], invw[64:128, 1] = 0.5, 0.25, 0.125, 0.0625
    pvec[:, PV_INVW:PV_INVW + 2] = invw
    gb = np.concatenate([inp["i_bias"][l], inp["f_bias"][l]]).astype(np.float32)
    gbias = np.ascontiguousarray(np.broadcast_to(np.tile(gb, 4)[None, :], (128, 32)))
    pw = inp["pool_w"][l]
    poolw = np.zeros((128, 2, 128), np.float32)
    for b in range(2):
        poolw[0:64, b, 0:64] = pw[2 * b]
        poolw[64:128, b, 64:128] = pw[2 * b + 1]
    return {f"wfm{l}": fm, f"wtm{l}": tm, f"wout{l}": wout, f"wup{l}": wup, f"wdn{l}": wdn,
            f"pvec{l}": pvec, f"gbias{l}": gbias, f"poolw{l}": np.ascontiguousarray(poolw.reshape(128, 256))}


FUSED = True


def kernel(**inputs):
    inp = {k: np.asarray(v, dtype=np.float32) for k, v in inputs.items()}
    x = inp["x"]
    cbf, cf32 = _consts()
    lay = [_prep_layer(inp, l) for l in range(2)]
    xT = [np.ascontiguousarray(x[b].T) for b in range(NCORES)]
    if FUSED:
        nc = build_program([0, 1])
        maps = []
        for b in range(NCORES):
            m = {"xT": xT[b], "cbf": cbf, "cf32": cf32}
            m.update(lay[0])
            m.update(lay[1])
            maps.append(m)
        res = run_bass_kernel_spmd(nc, maps, core_ids=list(range(NCORES)))
        yT = [res.results[b]["yT"] for b in range(NCORES)]
    else:
        cur = xT
        for l in range(2):
            nc = build_program([l])
            maps = []
            for b in range(NCORES):
                m = {"xT": cur[b], "cbf": cbf, "cf32": cf32}
                m.update(lay[l])
                maps.append(m)
            res = run_bass_kernel_spmd(nc, maps, core_ids=list(range(NCORES)))
            cur = [np.ascontiguousarray(res.results[b]["yT"]) for b in range(NCORES)]
        yT = cur
    out = np.stack([np.asarray(yT[b]).T for b in range(NCORES)], axis=0)
    return np.ascontiguousarray(out.astype(np.float32))
```
